# Optimizing a Trainium2 kernel written in Bass

```python
import math
import jax, jax.numpy as jnp
from jax import lax
import numpy as np

D_MODEL = 2048
BATCH = 8
SEQ = 4096
DEPTH = 2

N_A = max(1, DEPTH // 2)
N_B = DEPTH - N_A
A_HEAD_DIM = 128
A_HEADS = D_MODEL // A_HEAD_DIM
A_WIDTH = A_HEADS * A_HEAD_DIM
A_CHUNK = 64
B_HEAD_DIM = 128
B_HEADS = D_MODEL // (2 * B_HEAD_DIM)
B_WIDTH = B_HEADS * 2 * B_HEAD_DIM
Q_BLOCK = 128
D_FF = 4 * D_MODEL
PLE_DIM = 256
EPS = 1e-6

kernel_name = "yoco_hgrn2_diffattn_hybrid"


def rms_norm(x, gain):
    xf = x.astype(jnp.float32)
    y = xf * lax.rsqrt(jnp.mean(xf * xf, axis=-1, keepdims=True) + EPS)
    return (y * gain.astype(jnp.float32)).astype(x.dtype)


def hgrn2_chunkwise(q, k, v, log_f):
    b, s, h, dk = q.shape
    dv = v.shape[-1]
    n_chunks = s // A_CHUNK

    def to_chunks(t):
        return t.astype(jnp.float32).reshape(b, n_chunks, A_CHUNK, h, t.shape[-1]).transpose(1, 0, 3, 2, 4)

    qc, kc, vc, gc = (to_chunks(t) for t in (q, k, v, log_f))
    causal = jnp.tril(jnp.ones((A_CHUNK, A_CHUNK), dtype=bool))[:, :, None]

    def step(state, inp):
        qb, kb, vb, gb = inp
        g_cum = jnp.cumsum(gb, axis=2)
        rel = g_cum[:, :, :, None, :] - g_cum[:, :, None, :, :]
        decay = jnp.exp(jnp.where(causal, rel, -jnp.inf))
        scores = jnp.einsum('bhtd,bhsd,bhtsd->bhts', qb, kb, decay)
        o = (jnp.einsum('bhts,bhse->bhte', scores, vb)
             + jnp.einsum('bhtd,bhde->bhte', qb * jnp.exp(g_cum), state))
        g_last = g_cum[:, :, -1:, :]
        new_state = (jnp.exp(g_last)[:, :, 0, :, None] * state
                     + jnp.einsum('bhsd,bhse->bhde', kb * jnp.exp(g_last - g_cum), vb))
        return new_state, o

    state0 = jnp.zeros((b, h, dk, dv), jnp.float32)
    _, o = lax.scan(step, state0, (qc, kc, vc, gc))
    return o.transpose(1, 0, 3, 2, 4).reshape(b, s, h, dv)


def diff_attention(q, k, v, lam):
    b, s, h, _, dh = q.shape
    n_blocks = s // Q_BLOCK
    scale = 1.0 / math.sqrt(dh)
    kf = k.astype(jnp.float32)
    vf = v.astype(jnp.float32)
    key_pos = jnp.arange(s)
    q_blocks = q.astype(jnp.float32).reshape(b, n_blocks, Q_BLOCK, h, 2, dh).transpose(1, 0, 2, 3, 4, 5)

    def one_block(args):
        q_blk, blk = args
        logits = jnp.einsum('bqhcd,bkhcd->bhcqk', q_blk, kf) * scale
        q_pos = blk * Q_BLOCK + jnp.arange(Q_BLOCK)
        allowed = key_pos[None, :] <= q_pos[:, None]
        probs = jax.nn.softmax(jnp.where(allowed, logits, -jnp.inf), axis=-1)
        w = probs[:, :, 0] - lam * probs[:, :, 1]
        return jnp.einsum('bhqk,bkhe->bqhe', w, vf)

    out = lax.map(one_block, (q_blocks, jnp.arange(n_blocks)))
    return out.transpose(1, 0, 2, 3, 4).reshape(b, s, h, v.shape[-1]).astype(v.dtype)


def setup_inputs(seed: int = 0) -> dict:
    key = jax.random.key(seed)
    ks = jax.random.split(key, 25)

    def normal(k, shape, scale):
        return scale * jax.random.normal(k, shape, jnp.float32)

    def gain(k, shape):
        return 1.0 + 0.02 * jax.random.normal(k, shape, jnp.float32)

    out_scale = (2.0 * DEPTH) ** -0.5
    return {
        "x": normal(ks[0], (BATCH, SEQ, D_MODEL), 1.0),
        "p": normal(ks[1], (DEPTH, BATCH, SEQ, PLE_DIM), 1.0),
        "ln_mix": gain(ks[2], (DEPTH, D_MODEL)),
        "ln_mlp": gain(ks[3], (DEPTH, D_MODEL)),
        "ln_ple": gain(ks[4], (DEPTH, D_MODEL)),
        "a_w_in": normal(ks[5], (N_A, D_MODEL, 4 * A_WIDTH), D_MODEL ** -0.5),
        "a_lb": normal(ks[6], (DEPTH + 1, A_WIDTH), 0.1),
        "a_onorm": gain(ks[7], (N_A, A_HEAD_DIM)),
        "a_w_out": normal(ks[8], (N_A, A_WIDTH, D_MODEL), A_WIDTH ** -0.5 * out_scale),
        "kv_norm": gain(ks[9], (D_MODEL,)),
        "w_k": normal(ks[10], (D_MODEL, B_WIDTH), D_MODEL ** -0.5),
        "w_v": normal(ks[11], (D_MODEL, B_WIDTH), D_MODEL ** -0.5),
        "k_norm": gain(ks[12], (2, B_HEAD_DIM)),
        "b_w_q": normal(ks[13], (N_B, D_MODEL, B_WIDTH), D_MODEL ** -0.5),
        "q_norm": gain(ks[14], (N_B, 2, B_HEAD_DIM)),
        "lam_q1": normal(ks[15], (N_B, B_HEAD_DIM), 0.1),
        "lam_k1": normal(ks[16], (N_B, B_HEAD_DIM), 0.1),
        "lam_q2": normal(ks[17], (N_B, B_HEAD_DIM), 0.1),
        "lam_k2": normal(ks[18], (N_B, B_HEAD_DIM), 0.1),
        "b_subln": gain(ks[19], (N_B, 2 * B_HEAD_DIM)),
        "b_w_out": normal(ks[20], (N_B, B_WIDTH, D_MODEL), B_WIDTH ** -0.5 * out_scale),
        "mlp_up": normal(ks[21], (DEPTH, D_MODEL, D_FF), D_MODEL ** -0.5),
        "mlp_down": normal(ks[22], (DEPTH, D_FF, D_MODEL), D_FF ** -0.5 * out_scale),
        "ple_proj": normal(ks[23], (DEPTH, PLE_DIM, D_MODEL), PLE_DIM ** -0.5 * out_scale),
        "ple_gate": normal(ks[24], (DEPTH, D_MODEL, D_MODEL), D_MODEL ** -0.5),
    }


def reference(x, p, ln_mix, ln_mlp, ln_ple, a_w_in, a_lb, a_onorm, a_w_out, kv_norm, w_k, w_v, k_norm,
              b_w_q, q_norm, lam_q1, lam_k1, lam_q2, lam_k2, b_subln, b_w_out, mlp_up, mlp_down,
              ple_proj, ple_gate):
    b, s, _ = x.shape
    lower_bounds = jnp.cumsum(jax.nn.softmax(a_lb.astype(jnp.float32), axis=0), axis=0)
    k_shared = None
    v_shared = None
    for i in range(DEPTH):
        h = rms_norm(x, ln_mix[i])
        if i < N_A:
            j = i
            proj = h @ a_w_in[j]
            q_a, f_a, in_a, g_a = (t.reshape(b, s, A_HEADS, A_HEAD_DIM) for t in jnp.split(proj, 4, axis=-1))
            lb = lower_bounds[i].reshape(A_HEADS, A_HEAD_DIM)
            f = lb + (1.0 - lb) * jax.nn.sigmoid(f_a.astype(jnp.float32))
            o = hgrn2_chunkwise(jax.nn.silu(q_a), 1.0 - f, in_a, jnp.log(f))
            o = rms_norm(o, a_onorm[j]) * jax.nn.silu(g_a.astype(jnp.float32))
            x = x + o.reshape(b, s, A_WIDTH).astype(x.dtype) @ a_w_out[j]
        else:
            j = i - N_A
            q = rms_norm((h @ b_w_q[j]).reshape(b, s, B_HEADS, 2, B_HEAD_DIM), q_norm[j])
            lam_init = 0.8 - 0.6 * math.exp(-0.3 * i)
            lam = (jnp.exp(jnp.sum(lam_q1[j].astype(jnp.float32) * lam_k1[j].astype(jnp.float32)))
                   - jnp.exp(jnp.sum(lam_q2[j].astype(jnp.float32) * lam_k2[j].astype(jnp.float32)))
                   + lam_init)
            o = diff_attention(q, k_shared, v_shared, lam)
            o = rms_norm(o, b_subln[j]) * (1.0 - lam_init)
            x = x + o.reshape(b, s, B_WIDTH) @ b_w_out[j]
        h = rms_norm(x, ln_mlp[i])
        x = x + jnp.square(jax.nn.relu(h @ mlp_up[i])) @ mlp_down[i]
        gate = jax.nn.sigmoid(rms_norm(x, ln_ple[i]) @ ple_gate[i])
        x = x + gate * (p[i].astype(x.dtype) @ ple_proj[i])
        if i == N_A - 1:
            hk = rms_norm(x, kv_norm)
            k_shared = rms_norm((hk @ w_k).reshape(b, s, B_HEADS, 2, B_HEAD_DIM), k_norm)
            v_shared = (hk @ w_v).reshape(b, s, B_HEADS, 2 * B_HEAD_DIM)
    return x
```

```python
import math
from contextlib import ExitStack
import numpy as np
import concourse.bass as bass
import concourse.mybir as mybir
from concourse.bass_utils import run_bass_kernel_spmd

F32 = mybir.dt.float32
BF16 = mybir.dt.bfloat16
AF = mybir.ActivationFunctionType
ALU = mybir.AluOpType
AX = mybir.AxisListType

D = 2048
NCH = 16
TT = 512
DFF = 8192
EPS = 1e-6
WT = 4096
NSLOT = 6
CASTG = 4
LAM_INIT = 0.8 - 0.6 * math.exp(-0.3 * 1)
QSCALE = 1.0 / math.sqrt(128.0)

V_GAIN = 0
V_LB = 112
V_ONORM = 160
V_KN = 161
V_QN = 163
V_LAM = 165
V_SUBLN = 677
NV = 933
G_MIX0, G_MLP0, G_PLE0, G_KV, G_MIX1, G_MLP1, G_PLE1 = range(7)


class Sched:
    ENGS = ("pe", "act", "dve", "pool", "sp")

    def __init__(self):
        self.ops = []
        self.last_w = {}
        self.readers = {}

    def op(self, eng, fn, r=(), w=(), dma=None, waw_ok=False):
        i = len(self.ops)
        deps = set()
        for k in r:
            deps.update(self.last_w.get(k, ()))
        for k in w:
            rd = self.readers.get(k, [])
            lw = self.last_w.get(k, [])
            if rd:
                deps.update(rd)
                deps.update(lw)
                self.last_w[k] = [i]
                self.readers[k] = []
            elif waw_ok:
                self.last_w.setdefault(k, []).append(i)
            else:
                deps.update(lw)
                self.last_w[k] = [i]
        for k in r:
            if k not in w:
                self.readers.setdefault(k, []).append(i)
        deps.discard(i)
        self.ops.append(dict(eng=eng, fn=fn, deps=deps, dma=dma, isdma=dma is not None))
        return i

    def emit(self, sem_ctx):
        ops = self.ops
        n = len(ops)
        needed = [False] * n
        for o in ops:
            keep = set()
            best = {}
            for d in o["deps"]:
                od = ops[d]
                if od["isdma"]:
                    keep.add(d)
                    continue
                if od["eng"] == "pe" and o["eng"] == "pe" and not o["isdma"]:
                    continue
                if d > best.get(od["eng"], -1):
                    best[od["eng"]] = d
            keep.update(best.values())
            o["deps"] = keep
            for d in keep:
                needed[d] = True
        eng_cnt = {e: 0 for e in self.ENGS}
        dma_cnt = {}
        EPOCH = 4000
        DEPOCH = 200
        finals = {}
        for i, o in enumerate(ops):
            if o["isdma"]:
                s = o["dma"]
                c = dma_cnt.get(s, 0)
                dma_cnt[s] = c + 1
                name = f"dma:{s}:{c // DEPOCH}"
                o["sig"] = (name, (c % DEPOCH + 1) * 16)
                finals[name] = o["sig"][1]
            elif needed[i]:
                c = eng_cnt[o["eng"]]
                eng_cnt[o["eng"]] = c + 1
                o["sig"] = (f"eng:{o['eng']}:{c // EPOCH}", c % EPOCH + 1)
            else:
                o["sig"] = None
        self.final_counts = (eng_cnt, finals)
        streams = {e: [] for e in self.ENGS}
        for i, o in enumerate(ops):
            streams[o["eng"]].append(i)

        def run_stream(e, engobj):
            waited = {}
            for i in streams[e]:
                o = ops[i]
                need = {}
                for d in o["deps"]:
                    sname, val = ops[d]["sig"]
                    if val > need.get(sname, 0):
                        need[sname] = val
                for sname, val in need.items():
                    if waited.get(sname, 0) >= val:
                        continue
                    engobj.wait_ge(sem_ctx(sname), val)
                    waited[sname] = val
                ins = o["fn"](engobj)
                if o["sig"] is not None:
                    sname, val = o["sig"]
                    ins.then_inc(sem_ctx(sname), 16 if o["isdma"] else 1)

        return run_stream


def weight_tile_list():
    L = []
    for h in range(16):
        L.append(("ainA", h))
        L.append(("ainB", h))
    for t in range(8):
        L.append(("aout", t))
    for half in range(2):
        for t in range(16):
            L.append(("up0", half * 16 + t))
        for t in range(16):
            L.append(("dn0", half * 16 + t))
    L.append(("pp0", 0))
    for t in range(8):
        L.append(("pg0", t))
    for t in range(8):
        L.append(("wk", t))
    for t in range(8):
        L.append(("wv", t))
    for t in range(8):
        L.append(("bq", t))
    for t in range(8):
        L.append(("bout", t))
    for half in range(2):
        for t in range(16):
            L.append(("up1", half * 16 + t))
        for t in range(16):
            L.append(("dn1", half * 16 + t))
    L.append(("pp1", 0))
    for t in range(8):
        L.append(("pg1", t))
    return L


WLIST = weight_tile_list()
NT = len(WLIST)
NTP = ((NT + CASTG - 1) // CASTG) * CASTG


def _tile_kn(W, k0, kc, cols):
    sub = W[k0:k0 + kc * 128][:, cols]
    n = sub.shape[1]
    return np.ascontiguousarray(sub.reshape(kc, 128, n).transpose(1, 0, 2)).reshape(128, kc * n)


def host_weight_tiles(inp):
    out = np.zeros((NTP, 128, WT), np.float32)
    ar = np.arange
    for i, (nm, t) in enumerate(WLIST):
        if nm == "ainA":
            W = inp["a_w_in"][0]
            cols = np.concatenate([ar(t * 128, t * 128 + 128), ar(2048 + t * 128, 2048 + t * 128 + 128)])
            out[i] = _tile_kn(W, 0, 16, cols)
        elif nm == "ainB":
            W = inp["a_w_in"][0]
            cols = np.concatenate([ar(4096 + t * 128, 4096 + t * 128 + 128), ar(6144 + t * 128, 6144 + t * 128 + 128)])
            out[i] = _tile_kn(W, 0, 16, cols)
        elif nm in ("aout", "wk", "wv", "bq", "bout", "pg0", "pg1"):
            W = {"aout": inp["a_w_out"][0], "wk": inp["w_k"], "wv": inp["w_v"], "bq": inp["b_w_q"][0],
                 "bout": inp["b_w_out"][0], "pg0": inp["ple_gate"][0], "pg1": inp["ple_gate"][1]}[nm]
            out[i] = _tile_kn(W, 0, 16, ar(t * 256, t * 256 + 256))
        elif nm in ("up0", "up1"):
            W = inp["mlp_up"][int(nm[2])]
            out[i] = _tile_kn(W, 0, 16, ar(t * 256, t * 256 + 256))
        elif nm in ("dn0", "dn1"):
            W = inp["mlp_down"][int(nm[2])]
            half, m = divmod(t, 16)
            out[i] = _tile_kn(W, half * 4096, 32, ar(m * 128, m * 128 + 128))
        elif nm in ("pp0", "pp1"):
            W = inp["ple_proj"][int(nm[2])]
            out[i] = _tile_kn(W, 0, 2, ar(0, 2048))
    return out


def host_vecs(inp):
    v = np.zeros((128, NV), np.float32)
    gains = [inp["ln_mix"][0], inp["ln_mlp"][0], inp["ln_ple"][0], inp["kv_norm"],
             inp["ln_mix"][1], inp["ln_mlp"][1], inp["ln_ple"][1]]
    for i, g in enumerate(gains):
        v[:, V_GAIN + i * 16:V_GAIN + (i + 1) * 16] = g.reshape(16, 128).T
    for r in range(3):
        v[:, V_LB + r * 16:V_LB + (r + 1) * 16] = inp["a_lb"][r].reshape(16, 128).T
    v[:, V_ONORM] = inp["a_onorm"][0]
    v[:, V_KN] = inp["k_norm"][0]
    v[:, V_KN + 1] = inp["k_norm"][1]
    v[:, V_QN] = inp["q_norm"][0, 0]
    v[:, V_QN + 1] = inp["q_norm"][0, 1]
    lam = np.concatenate([inp["lam_q1"][0], inp["lam_k1"][0], inp["lam_q2"][0], inp["lam_k2"][0]])
    v[:, V_LAM:V_LAM + 512] = lam[None, :]
    v[:, V_SUBLN:V_SUBLN + 256] = inp["b_subln"][0][None, :]
    return v


def build_nc(S, stop_after=None):
    import os
    HSTOP = int(os.environ.get('DBG_HSTOP', 99))
    NTT = S // TT
    nc = bass.Bass("TRN2", target_bir_lowering=False)
    xT_d = nc.dram_tensor("xT", [D, S], F32, kind="ExternalInput").ap()
    pT_d = nc.dram_tensor("pT", [2, 256, S], F32, kind="ExternalInput").ap()
    wf_d = nc.dram_tensor("wf", [NTP, 128, WT], F32, kind="ExternalInput").ap()
    vec_d = nc.dram_tensor("vecs", [128, NV], F32, kind="ExternalInput").ap()
    out_d = nc.dram_tensor("outT", [D, S], F32, kind="ExternalOutput").ap()
    wb_d = nc.dram_tensor("wb", [NTP, 128, WT], BF16, kind="Internal").ap()
    kt_d = nc.dram_tensor("ktc", [NTT, 128, 16 * TT], BF16, kind="Internal").ap()
    v_d = nc.dram_tensor("vc", [NTT, 128, 8 * 4 * 257], BF16, kind="Internal").ap()

    xT_v = xT_d.rearrange("(c p) t -> p c t", p=128)
    out_v = out_d.rearrange("(c p) t -> p c t", p=128)

    S_ = Sched()
    op = S_.op
    es = ExitStack()
    with es:
        def sb(name, shape, dt):
            return es.enter_context(nc.sbuf_tensor(name, shape, dt))

        xT = sb("xTs", [128, NCH, TT], F32)
        hT = sb("hT", [128, NCH, TT], BF16)
        sqb = sb("sqb", [128, 4, TT], BF16)
        rstd = sb("rstd", [128, TT], F32)
        wr = sb("wring", [128, NSLOT, WT], BF16)
        wpj = sb("wpj", [128, WT], BF16)
        act = sb("actT", [128, 32, TT], BF16)
        U = sb("U", [128, 20, TT], F32)
        Scar = sb("Scar", [128, 16, 128], F32)
        vecs = sb("vecs_s", [128, NV], F32)
        pTb = sb("pTb", [128, 2, TT], BF16)
        pTf = sb("pTf", [128, 2, TT], F32)
        ident = sb("ident", [128, 128], BF16)
        ones = sb("ones", [128, 128], BF16)
        causal = sb("causal", [128, 128], BF16)
        maskbd = sb("maskbd", [128, TT], F32)
        scanm = sb("scanm", [128, TT], F32)
        small = sb("small", [128, 64], F32)
        epsc = sb("epsc", [128, 1], F32)
        hmask = sb("hmask", [128, 2], F32)
        lbt = sb("lbt", [128, 48], F32)
        lamt = sb("lamt", [128, 256], F32)
        sublnb = sb("sublnb", [128, 256], F32)
        ps = [es.enter_context(nc.psum_tensor(f"ps{i}", [128, 512], F32)) for i in range(8)]

        sems = {}

        def sem(name):
            if name not in sems:
                sems[name] = es.enter_context(nc.semaphore(name.replace(":", "_")))
            return sems[name]

        def Uf(i, n=1):
            if n == 1:
                return U[:, i, :]
            return U[:, i:i + n, :]

        def Ub(i):
            return U[:, i, :].bitcast(BF16)

        UK = lambda i: ("U", i)
        PK = lambda i: ("ps", i)
        AK = lambda i: ("act", i)

        op("sp", lambda e: e.dma_start(out=vecs[:], in_=vec_d), w=["vecs"], dma="vec")
        actf = act[:].rearrange("p a t -> p (a t)").bitcast(F32)
        Uflat = U[:].rearrange("p a t -> p (a t)")
        stages = [(Uflat[:, 0:WT], [("U", i) for i in range(0, 8)]),
                  (Uflat[:, WT:2 * WT], [("U", i) for i in range(8, 16)]),
                  (actf[:, 0:WT], [("act", i) for i in range(0, 16)]),
                  (actf[:, WT:2 * WT], [("act", i) for i in range(16, 32)])]
        import os
        ncast = int(os.environ.get("DBG_NCAST", NT))
        def cast_ld(i):
            stg, skeys = stages[i % 4]
            op("sp", lambda e: e.dma_start(out=stg, in_=wf_d[i]), w=skeys, dma=f"cl{i % 4}")
        for i in range(min(4, ncast)):
            cast_ld(i)
        for i in range(ncast):
            stg, skeys = stages[i % 4]
            slot = i % NSLOT
            if i % 2 == 0:
                op("act", lambda e, stg=stg, slot=slot: e.activation(out=wr[:, slot, :], in_=stg, func=AF.Copy), r=skeys, w=[("w", slot)])
            else:
                op("dve", lambda e, stg=stg, slot=slot: e.tensor_copy(out=wr[:, slot, :], in_=stg), r=skeys, w=[("w", slot)])
            op("sp", lambda e, i=i, slot=slot: e.dma_start(out=wb_d[i], in_=wr[:, slot, :]), r=[("w", slot)], w=[("wb", i)], dma=f"cs{slot}")
            if i + 4 < ncast:
                cast_ld(i + 4)
        op("dve", lambda e: e.memset(ones[:], 1.0), w=["ones"])
        op("dve", lambda e: e.memset(epsc[:], EPS), w=["epsc"])
        op("dve", lambda e: e.memset(ident[:], 1.0), w=["ident"])
        op("pool", lambda e: e.affine_select(out=ident[:], in_=ident[:], pattern=[[-1, 128]], compare_op=ALU.is_equal,
                                              fill=0.0, base=0, channel_multiplier=1), r=["ident"], w=["ident"])
        op("dve", lambda e: e.memset(causal[:], 1.0), w=["causal"])
        op("pool", lambda e: e.affine_select(out=causal[:], in_=causal[:], pattern=[[1, 128]], compare_op=ALU.is_ge,
                                              fill=0.0, base=0, channel_multiplier=-1), r=["causal"], w=["causal"])
        op("dve", lambda e: e.memset(maskbd[:], 1.0), w=["maskbd"])
        mb3 = maskbd[:].rearrange("p (j t) -> p j t", t=128)
        op("pool", lambda e: e.affine_select(out=mb3, in_=mb3, pattern=[[0, 4], [1, 128]], compare_op=ALU.is_ge,
                                              fill=0.0, base=0, channel_multiplier=-1), r=["maskbd"], w=["maskbd"])
        op("dve", lambda e: e.memset(scanm[:], 1.0), w=["scanm"])
        sm3 = scanm[:].rearrange("p (c t) -> p c t", t=128)
        op("dve", lambda e: e.memset(sm3[:, :, 0:1], 0.0), r=["scanm"], w=["scanm"])
        op("dve", lambda e: e.memset(Scar[:], 0.0), w=["Scar"])
        op("act", lambda e: e.activation(out=lbt[:], in_=vecs[:, V_LB:V_LB + 48], func=AF.Exp), r=["vecs"], w=["lbt"])
        op("dve", lambda e: e.tensor_tensor(out=small[:, 32:48], in0=lbt[:, 0:16], in1=lbt[:, 16:32], op=ALU.add), r=["lbt"], w=["small"])
        op("dve", lambda e: e.tensor_tensor(out=small[:, 32:48], in0=small[:, 32:48], in1=lbt[:, 32:48], op=ALU.add), r=["lbt", "small"], w=["small"])
        op("dve", lambda e: e.reciprocal(out=small[:, 32:48], in_=small[:, 32:48]), r=["small"], w=["small"])
        op("dve", lambda e: e.tensor_tensor(out=small[:, 0:16], in0=lbt[:, 0:16], in1=small[:, 32:48], op=ALU.mult), r=["lbt", "small"], w=["small"])
        op("dve", lambda e: e.tensor_scalar(out=small[:, 16:32], in0=small[:, 0:16], scalar1=-1.0, scalar2=1.0, op0=ALU.mult, op1=ALU.add), r=["small"], w=["small"])
        op("dve", lambda e: e.tensor_tensor(out=lamt[:, 0:128], in0=vecs[:, V_LAM:V_LAM + 128], in1=vecs[:, V_LAM + 128:V_LAM + 256], op=ALU.mult), r=["vecs"], w=["lamt"])
        op("dve", lambda e: e.tensor_tensor(out=lamt[:, 128:256], in0=vecs[:, V_LAM + 256:V_LAM + 384], in1=vecs[:, V_LAM + 384:V_LAM + 512], op=ALU.mult), r=["vecs", "lamt"], w=["lamt"])
        op("dve", lambda e: e.reduce_sum(out=small[:, 49:51], in_=lamt[:].rearrange("p (a b) -> p a b", b=128), axis=AX.X), r=["lamt", "small"], w=["small"])
        op("act", lambda e: e.activation(out=small[:, 49:51], in_=small[:, 49:51], func=AF.Exp), r=["small"], w=["small"])
        op("dve", lambda e: e.tensor_tensor(out=small[:, 48:49], in0=small[:, 50:51], in1=small[:, 49:50], op=ALU.subtract), r=["small"], w=["small"])
        op("dve", lambda e: e.tensor_scalar(out=small[:, 48:49], in0=small[:, 48:49], scalar1=-LAM_INIT, scalar2=None, op0=ALU.add), r=["small"], w=["small"])
        op("dve", lambda e: e.tensor_scalar(out=sublnb[:], in0=vecs[:, V_SUBLN:V_SUBLN + 256], scalar1=1.0 - LAM_INIT, scalar2=None, op0=ALU.mult), r=["vecs"], w=["sublnb"])
        LB = lambda h: small[:, h:h + 1]
        OML = lambda h: small[:, 16 + h:17 + h]
        NEGLAM = small[:, 48:49]

        wstate = dict(issued=0)

        def w_issue_upto(n):
            while wstate["issued"] <= n:
                q = wstate["issued"]
                gi = q % NT
                if WLIST[gi][0] in ("pp0", "pp1"):
                    wstate["issued"] += 1
                    continue
                slot = q % NSLOT
                op("sp", lambda e, gi=gi, slot=slot: e.dma_start(out=wr[:, slot, :], in_=wb_d[gi]),
                   r=[("wb", gi)], w=[("w", slot)], dma=f"w{slot}")
                wstate["issued"] += 1

        def w_get(q):
            w_issue_upto(min(q + NSLOT - 1, NTT * NT - 1))
            return q % NSLOT

        ctr = dict(bank=0, alt=0)

        def alt(a, b):
            ctr["alt"] += 1
            return a if ctr["alt"] % 2 else b

        def linear_fm(q0, ntiles, mper, kc, src, evac, banks=(0, 1, 2, 3)):
            ncols = WT // kc
            for ti in range(ntiles):
                slot = w_get(q0 + ti)
                for j in range(mper):
                    m = ti * mper + j
                    bank = banks[ctr["bank"] % len(banks)]
                    ctr["bank"] += 1
                    for k in range(kc):
                        rhs, key = src(k)
                        op("pe", lambda e, bank=bank, slot=slot, k=k, j=j, rhs=rhs: e.matmul(
                            ps[bank][:], lhsT=wr[:, slot, k * ncols + j * 128:k * ncols + (j + 1) * 128], rhs=rhs,
                            start=(k == 0), stop=(k == kc - 1)), r=[("w", slot), key], w=[PK(bank)])
                    evac(m, bank)

        def norm_fm(gidx):
            for c in range(NCH):
                s = c % 4
                if c % 2 == 0:
                    op("act", lambda e, c=c, s=s: e.activation(out=sqb[:, s, :], in_=xT[:, c, :], func=AF.Square),
                       r=[("x", c)], w=[("sqb", s)])
                else:
                    op("pool", lambda e, c=c, s=s: e.tensor_tensor(out=sqb[:, s, :], in0=xT[:, c, :], in1=xT[:, c, :], op=ALU.mult),
                       r=[("x", c)], w=[("sqb", s)])
                op("pe", lambda e, c=c, s=s: e.matmul(ps[4][:], lhsT=ones[:], rhs=sqb[:, s, :], start=(c == 0), stop=(c == NCH - 1)),
                   r=["ones", ("sqb", s)], w=[PK(4)])
            op("act", lambda e: e.activation(out=rstd[:], in_=ps[4][:], func=AF.Ln, scale=1.0 / D, bias=epsc[:, 0:1]), r=[PK(4), "epsc"], w=["rstd"])
            op("act", lambda e: e.activation(out=rstd[:], in_=rstd[:], func=AF.Exp, scale=-0.5), r=["rstd"], w=["rstd"])
            for c in range(NCH):
                col = V_GAIN + gidx * 16 + c
                op("dve", lambda e, c=c, col=col: e.scalar_tensor_tensor(
                    out=hT[:, c, :], in0=xT[:, c, :], scalar=vecs[:, col:col + 1], in1=rstd[:], op0=ALU.mult, op1=ALU.mult),
                   r=[("x", c), "vecs", "rstd"], w=[("h", c)])

        def src_h(k):
            return hT[:, k, :], ("h", k)

        def src_act(k):
            return act[:, k, :], AK(k)

        def evac_resid(m, bank):
            op("dve", lambda e: e.tensor_tensor(out=xT[:, m, :], in0=xT[:, m, :], in1=ps[bank][:], op=ALU.add),
               r=[PK(bank), ("x", m)], w=[("x", m)])

        def hgrn_pieces(q0, h):
            par = h % 2
            f_, k_, lg_, q_, _, e1_, dm_, eq_, o_, G_ = [Uf(i) for i in range(10)]
            sg_ = Uf(4) if par == 0 else Uf(17)
            SGK = UK(4) if par == 0 else UK(17)
            qt = Ub(10)[:, 0:TT]
            kt = Ub(10)[:, TT:2 * TT]
            khat = Ub(11)[:, 0:TT]
            khatT = Ub(11)[:, TT:2 * TT]
            scm = Ub(12)[:, 0:TT]
            vtok = Ub(12)[:, TT:2 * TT] if par == 0 else Ub(14)[:, TT:2 * TT]
            VK = ("vtok", par)
            osq = Ub(13)[:, 0:TT]
            qS = Ub(13)[:, TT:2 * TT]
            Sb = Ub(14)[:, 0:TT]
            Sall = U[:, 15:17, :].rearrange("p a t -> p (a t)")
            egl = small[:, 52:56]
            e3_ = lg_
            st = {}

            def mm_fm(bank, slot, colofs):
                for k in range(NCH):
                    op("pe", lambda e, k=k: e.matmul(ps[bank][:], lhsT=wr[:, slot, k * 256 + colofs:k * 256 + colofs + 128],
                                                     rhs=hT[:, k, :], start=(k == 0), stop=(k == NCH - 1)),
                       r=[("w", slot), ("h", k)], w=[PK(bank)])

            def Pq():
                st["sA"] = w_get(q0 + 2 * h)
                st["sB"] = (q0 + 2 * h + 1) % NSLOT
                mm_fm(0, st["sA"], 0)
                op("act", lambda e: e.activation(out=q_, in_=ps[0][:], func=AF.Exp, scale=-1.0), r=[PK(0)], w=[UK(3)])
                op("pool", lambda e: e.tensor_scalar(out=q_, in0=q_, scalar1=1.0, scalar2=None, op0=ALU.add), r=[UK(3)], w=[UK(3)])
                op("dve", lambda e: e.reciprocal(out=q_, in_=q_), r=[UK(3)], w=[UK(3)])
                op("dve", lambda e: e.tensor_tensor(out=q_, in0=q_, in1=ps[0][:], op=ALU.mult), r=[UK(3), PK(0)], w=[UK(3)])

            def Pf():
                mm_fm(1, st["sA"], 128)
                op("act", lambda e: e.activation(out=f_, in_=ps[1][:], func=AF.Exp, scale=-1.0), r=[PK(1)], w=[UK(0)])
                op("pool", lambda e: e.tensor_scalar(out=f_, in0=f_, scalar1=1.0, scalar2=None, op0=ALU.add), r=[UK(0)], w=[UK(0)])
                op("dve", lambda e: e.reciprocal(out=f_, in_=f_), r=[UK(0)], w=[UK(0)])

            def Pv():
                sB = st["sB"]
                for sub in range(4):
                    for k in range(NCH):
                        op("pe", lambda e, k=k, sub=sub: e.matmul(ps[3][:, sub * 128:(sub + 1) * 128], lhsT=hT[:, k, sub * 128:(sub + 1) * 128],
                                                                  rhs=wr[:, sB, k * 256:k * 256 + 128], start=(k == 0), stop=(k == NCH - 1)),
                           r=[("w", sB), ("h", k)], w=[PK(3)])
                op("act", lambda e: e.activation(out=vtok, in_=ps[3][:], func=AF.Copy), r=[PK(3)], w=[VK])

            def Pg():
                mm_fm(1, st["sB"], 128)
                op("act", lambda e: e.activation(out=sg_, in_=ps[1][:], func=AF.Exp, scale=-1.0), r=[PK(1)], w=[SGK])
                op("pool", lambda e: e.tensor_scalar(out=sg_, in0=sg_, scalar1=1.0, scalar2=None, op0=ALU.add), r=[SGK], w=[SGK])
                op("dve", lambda e: e.reciprocal(out=sg_, in_=sg_), r=[SGK], w=[SGK])
                op("dve", lambda e: e.tensor_tensor(out=sg_, in0=sg_, in1=ps[1][:], op=ALU.mult), r=[SGK, PK(1)], w=[SGK])

            def RA():
                op("pool", lambda e: e.tensor_scalar(out=f_, in0=f_, scalar1=OML(h), scalar2=LB(h), op0=ALU.mult, op1=ALU.add),
                   r=[UK(0), "small"], w=[UK(0)])
                op("pool", lambda e: e.tensor_scalar(out=k_, in0=f_, scalar1=-1.0, scalar2=1.0, op0=ALU.mult, op1=ALU.add),
                   r=[UK(0)], w=[UK(1)])
                op("act", lambda e: e.activation(out=lg_, in_=f_, func=AF.Ln), r=[UK(0)], w=[UK(2)])
                op("dve", lambda e: e.tensor_tensor_scan(out=G_, data0=scanm[:], data1=lg_, initial=0.0, op0=ALU.mult, op1=ALU.add),
                   r=["scanm", UK(2)], w=[UK(9)])
                G3 = G_.rearrange("p (c t) -> p c t", t=128)
                op("act", lambda e: e.activation(out=e1_, in_=G_, func=AF.Exp), r=[UK(9)], w=[UK(5)])
                dm3 = dm_.rearrange("p (c t) -> p c t", t=128)
                op("pool", lambda e: e.tensor_tensor(out=dm3, in0=G3, in1=G3[:, :, 63:64].to_broadcast([128, 4, 128]), op=ALU.subtract),
                   r=[UK(9)], w=[UK(6)])
                op("act", lambda e: e.activation(out=eq_, in_=dm_, func=AF.Exp), r=[UK(6)], w=[UK(7)])
                op("act", lambda e: e.activation(out=dm_, in_=dm_, func=AF.Exp, scale=-1.0), r=[UK(6), UK(7)], w=[UK(6)])
                e33 = e3_.rearrange("p (c t) -> p c t", t=128)
                op("pool", lambda e: e.tensor_tensor(out=e33, in0=G3[:, :, 127:128].to_broadcast([128, 4, 128]), in1=G3, op=ALU.subtract),
                   r=[UK(9), UK(2)], w=[UK(2)])
                op("act", lambda e: e.activation(out=e3_, in_=e3_, func=AF.Exp), r=[UK(2)], w=[UK(2)])
                op("act", lambda e: e.activation(out=egl.unsqueeze(2), in_=G3[:, :, 127:128], func=AF.Exp), r=[UK(9)], w=["egl"])
                op("dve", lambda e: e.tensor_tensor(out=qS, in0=q_, in1=e1_, op=ALU.mult), r=[UK(3), UK(5)], w=[("qS",)])
                op("dve", lambda e: e.tensor_tensor(out=qt, in0=q_, in1=eq_, op=ALU.mult), r=[UK(3), UK(7)], w=[("qt",)])
                op("pool", lambda e: e.tensor_tensor(out=kt, in0=k_, in1=dm_, op=ALU.mult), r=[UK(1), UK(6)], w=[("kt",)])
                op("pool", lambda e: e.tensor_tensor(out=khat, in0=k_, in1=e3_, op=ALU.mult), r=[UK(1), UK(2)], w=[("khat",)])

            def RB():
                p5b = ps[5][:].bitcast(BF16)
                for j in range(4):
                    op("pe", lambda e, j=j: e.transpose(p5b[:, j * 128:(j + 1) * 128], khat[:, j * 128:(j + 1) * 128], ident[:]),
                       r=[("khat",), "ident"], w=[PK(5)])
                op("act", lambda e: e.activation(out=khatT, in_=p5b[:, 0:TT], func=AF.Copy), r=[PK(5)], w=[("khatT",)])

            def RC():
                for j in range(4):
                    op("pe", lambda e, j=j: e.matmul(ps[6][:, j * 128:(j + 1) * 128], lhsT=kt[:, j * 128:(j + 1) * 128],
                                                     rhs=qt[:, j * 128:(j + 1) * 128], start=True, stop=True),
                       r=[("kt",), ("qt",)], w=[PK(6)])
                op("dve", lambda e: e.tensor_tensor(out=scm, in0=ps[6][:], in1=maskbd[:], op=ALU.mult), r=[PK(6), "maskbd"], w=[("scm",)])

            def RD():
                for j in range(4):
                    op("pe", lambda e, j=j: e.matmul(ps[2][:, j * 128:(j + 1) * 128], lhsT=khatT[:, j * 128:(j + 1) * 128],
                                                     rhs=vtok[:, j * 128:(j + 1) * 128], start=True, stop=True),
                       r=[("khatT",), VK], w=[PK(2)])
                op("pool", lambda e: e.tensor_copy(out=Sall[:, 0:128], in_=Scar[:, h, :]), r=[("Scar", h)], w=[("Sall", 0)])
                for c in range(4):
                    op("dve", lambda e, c=c: e.scalar_tensor_tensor(
                        out=Sall[:, (c + 1) * 128:(c + 2) * 128], in0=Sall[:, c * 128:(c + 1) * 128], scalar=egl[:, c:c + 1],
                        in1=ps[2][:, c * 128:(c + 1) * 128], op0=ALU.mult, op1=ALU.add),
                       r=[("Sall", c), "egl", PK(2)], w=[("Sall", c + 1)])
                op("pool", lambda e: e.tensor_copy(out=Scar[:, h, :], in_=Sall[:, 512:640]), r=[("Sall", 4)], w=[("Scar", h)])
                op("act", lambda e: e.activation(out=Sb, in_=Sall[:, 0:512], func=AF.Copy), r=[("Sall", c) for c in range(4)], w=[("Sb",)])

            def RE():
                for j in range(4):
                    op("pe", lambda e, j=j: e.matmul(ps[7][:, j * 128:(j + 1) * 128], lhsT=vtok[:, j * 128:(j + 1) * 128],
                                                     rhs=scm[:, j * 128:(j + 1) * 128], start=True, stop=False),
                       r=[VK, ("scm",)], w=[PK(7)])
                    op("pe", lambda e, j=j: e.matmul(ps[7][:, j * 128:(j + 1) * 128], lhsT=Sb[:, j * 128:(j + 1) * 128],
                                                     rhs=qS[:, j * 128:(j + 1) * 128], start=False, stop=True),
                       r=[("Sb",), ("qS",)], w=[PK(7)])
                op("dve", lambda e: e.tensor_copy(out=o_, in_=ps[7][:]), r=[PK(7)], w=[UK(8)])
                op("act", lambda e: e.activation(out=osq, in_=o_, func=AF.Square), r=[UK(8)], w=[("osq",)])
                op("pe", lambda e: e.matmul(ps[4][:], lhsT=ones[:], rhs=osq, start=True, stop=True), r=["ones", ("osq",)], w=[PK(4)])
                op("act", lambda e: e.activation(out=G_, in_=ps[4][:], func=AF.Ln, scale=1.0 / 128, bias=epsc[:, 0:1]), r=[PK(4), "epsc"], w=[UK(9)])
                op("act", lambda e: e.activation(out=G_, in_=G_, func=AF.Exp, scale=-0.5), r=[UK(9)], w=[UK(9)])
                op("dve", lambda e: e.scalar_tensor_tensor(out=o_, in0=o_, scalar=vecs[:, V_ONORM:V_ONORM + 1], in1=G_, op0=ALU.mult, op1=ALU.mult),
                   r=[UK(8), UK(9), "vecs"], w=[UK(8)])
                op("pool", lambda e: e.tensor_tensor(out=act[:, h, :], in0=o_, in1=sg_, op=ALU.mult), r=[UK(8), SGK], w=[AK(h)])

            return (Pq, Pf, Pv, Pg), (RA, RB, RC, RD, RE)

        def hgrn_all(q0, nh):
            pcs = [hgrn_pieces(q0, h) for h in range(nh)]
            for h in range(nh):
                P, R = pcs[h]
                if h == 0:
                    for p in P:
                        p()
                R[0]()
                if h + 1 < nh:
                    Pn = pcs[h + 1][0]
                    Pn[0](); R[1](); Pn[1](); R[2](); Pn[2](); R[3](); Pn[3](); R[4]()
                else:
                    R[1](); R[2](); R[3](); R[4]()

        def mlp(q0, gidx):
            norm_fm(gidx)
            for half in range(2):
                def evac_up(m, bank):
                    t = m % 2
                    op("act", lambda e: e.activation(out=Uf(t), in_=ps[bank][:], func=AF.Relu), r=[PK(bank)], w=[UK(t)])
                    op("pool", lambda e: e.tensor_tensor(out=act[:, m, :], in0=Uf(t), in1=Uf(t), op=ALU.mult), r=[UK(t)], w=[AK(m)])
                linear_fm(q0 + half * 32, 16, 2, 16, src_h, evac_up)
                linear_fm(q0 + half * 32 + 16, 16, 1, 32, src_act, evac_resid)

        def ple(q0, gidx, layer, t0):
            gpp = q0 % NT
            op("sp", lambda e: e.dma_start(out=wpj[:], in_=wb_d[gpp]), r=[("wb", gpp)], w=["wpj"], dma="wpj")
            op("sp", lambda e: e.dma_start(out=pTf[:], in_=pT_d[layer].rearrange("(k p) t -> p k t", p=128)[:, :, t0:t0 + TT]),
               w=["pTf"], dma="pt")
            op("pool", lambda e: e.tensor_copy(out=pTb[:], in_=pTf[:]), r=["pTf"], w=["pTb"])
            norm_fm(gidx)

            def evac_gate(m, bank):
                t = 2 + m % 2
                op("act", lambda e: e.activation(out=Uf(t), in_=ps[bank][:], func=AF.Exp, scale=-1.0), r=[PK(bank)], w=[UK(t)])
                op("pool", lambda e: e.tensor_scalar(out=Uf(t), in0=Uf(t), scalar1=1.0, scalar2=None, op0=ALU.add), r=[UK(t)], w=[UK(t)])
                op("dve", lambda e: e.reciprocal(out=Uf(t), in_=Uf(t)), r=[UK(t)], w=[UK(t)])
                pb = 5 + m % 2
                for k in range(2):
                    op("pe", lambda e, k=k: e.matmul(ps[pb][:], lhsT=wpj[:, k * 2048 + m * 128:k * 2048 + (m + 1) * 128], rhs=pTb[:, k, :],
                                                     start=(k == 0), stop=(k == 1)), r=["wpj", "pTb"], w=[PK(pb)])
                op("dve", lambda e: e.tensor_tensor(out=Uf(t), in0=Uf(t), in1=ps[pb][:], op=ALU.mult), r=[UK(t), PK(pb)], w=[UK(t)])
                op("pool", lambda e: e.tensor_tensor(out=xT[:, m, :], in0=xT[:, m, :], in1=Uf(t), op=ALU.add), r=[UK(t), ("x", m)], w=[("x", m)])
            linear_fm(q0 + 1, 8, 2, 16, src_h, evac_gate)

        def headnorm_evac(bank, dst, dkey, ncol, t):
            sq = sqb[:, t, :]
            op("dve", lambda e: e.tensor_copy(out=Uf(18 + t), in_=ps[bank][:]), r=[PK(bank)], w=[UK(18 + t)])
            op("act", lambda e: e.activation(out=sq, in_=Uf(18 + t), func=AF.Square), r=[UK(18 + t)], w=[("sqb", t)])
            op("pe", lambda e: e.matmul(ps[4][:], lhsT=ones[:], rhs=sq, start=True, stop=True), r=["ones", ("sqb", t)], w=[PK(4)])
            op("act", lambda e: e.activation(out=rstd[:], in_=ps[4][:], func=AF.Ln, scale=1.0 / 128, bias=epsc[:, 0:1]), r=[PK(4), "epsc"], w=["rstd"])
            op("act", lambda e: e.activation(out=rstd[:], in_=rstd[:], func=AF.Exp, scale=-0.5), r=["rstd"], w=["rstd"])
            op("dve", lambda e: e.scalar_tensor_tensor(out=dst, in0=Uf(18 + t), scalar=vecs[:, ncol:ncol + 1], in1=rstd[:], op0=ALU.mult, op1=ALU.mult),
               r=[UK(18 + t), "rstd", "vecs"], w=[dkey])

        VSTK = [UK(i) for i in range(9)] + [("KTb", 0), ("KTb", 1)]

        def kv_stage(q0, tt):
            norm_fm(G_KV)

            def evac_k(m, bank):
                headnorm_evac(bank, act[:, m, :], AK(m), V_KN + (m % 2), m % 2)
            linear_fm(q0, 8, 2, 16, src_h, evac_k, banks=(0, 1))
            op("sp", lambda e: e.dma_start(out=kt_d[tt], in_=act[:, 0:16, :].rearrange("p a t -> p (a t)")),
               r=[AK(m) for m in range(16)], w=[("ktd", tt)], dma="kts")
            Vst = U[:, 0:9, :].rearrange("p a t -> p (a t)").bitcast(BF16)[:, 0:8 * 4 * 257].rearrange("p (h s e) -> p h s e", h=8, s=4)
            op("pool", lambda e: e.memset(Vst[:, :, :, 256:257], 1.0), w=VSTK)
            for hd in range(8):
                slot = w_get(q0 + 8 + hd)
                for sub in range(4):
                    bank = (0, 1, 2, 3)[ctr["bank"] % 4]
                    ctr["bank"] += 1
                    for k in range(NCH):
                        op("pe", lambda e, k=k, sub=sub, bank=bank, slot=slot: e.matmul(
                            ps[bank][:, 0:256], lhsT=hT[:, k, sub * 128:(sub + 1) * 128], rhs=wr[:, slot, k * 256:(k + 1) * 256],
                            start=(k == 0), stop=(k == NCH - 1)), r=[("w", slot), ("h", k)], w=[PK(bank)])
                    if ctr["bank"] % 2:
                        op("dve", lambda e, hd=hd, sub=sub, bank=bank: e.tensor_copy(out=Vst[:, hd, sub, 0:256], in_=ps[bank][:, 0:256]),
                           r=[PK(bank)], w=VSTK, waw_ok=True)
                    else:
                        op("act", lambda e, hd=hd, sub=sub, bank=bank: e.activation(out=Vst[:, hd, sub, 0:256], in_=ps[bank][:, 0:256], func=AF.Copy),
                           r=[PK(bank)], w=VSTK, waw_ok=True)
            op("sp", lambda e: e.dma_start(out=v_d[tt], in_=U[:, 0:9, :].rearrange("p a t -> p (a t)").bitcast(BF16)[:, 0:8 * 4 * 257]),
               r=VSTK, w=[("vd", tt)], dma="vs")

        def attn_stage(q0, tt):
            norm_fm(G_MIX1)

            def evac_q(m, bank):
                headnorm_evac(bank, act[:, 16 + m, :], AK(16 + m), V_QN + (m % 2), m % 2)
            linear_fm(q0, 8, 2, 16, src_h, evac_q, banks=(0, 1))
            KTb = [Ub(0)[:, 0:TT], Ub(0)[:, TT:2 * TT]]
            Vb = [U[:, 2:4, :].rearrange("p a t -> p (a t)").bitcast(BF16)[:, 0:4 * 257].rearrange("p (s e) -> p s e", s=4),
                  U[:, 4:6, :].rearrange("p a t -> p (a t)").bitcast(BF16)[:, 0:4 * 257].rearrange("p (s e) -> p s e", s=4)]
            VbK = [[UK(2), UK(3)], [UK(4), UK(5)]]
            probs = [Ub(6)[:, 0:TT], Ub(6)[:, TT:2 * TT]]
            o1 = U[:, 7:9, :].rearrange("p a t -> p (a t)").rearrange("p (q e) -> p q e", q=4)
            o2 = U[:, 9:11, :].rearrange("p a t -> p (a t)").rearrange("p (q e) -> p q e", q=4)
            yb = Ub(11).rearrange("p (q e) -> p q e", q=4)
            rden = small[:, 60:64]
            ssq = small[:, 40:44]
            obank = (2, 3, 5, 6)
            kv_d4 = kt_d.rearrange("n p (m t) -> n p m t", m=16)
            v_d4 = v_d.rearrange("n p (h x) -> n p h x", h=8)
            cnt = dict(b=0, l=0)
            p7b = ps[7][:].bitcast(BF16).rearrange("p (a t) -> p a t", a=2)
            iters = [(hd, c, j, kb) for hd in range(8) for c in range(2) for j in range(tt + 1) for kb in range(4)]
            cur = dict(b=0)

            def emit_L(it):
                hd, c, j, kb = it
                m = 2 * hd + c
                if kb == 0:
                    b = cnt["b"] % 2
                    cnt["b"] += 1
                    cur["b"] = b
                    op("sp", lambda e: e.dma_start(out=KTb[b], in_=kv_d4[j, :, m, :]),
                       r=[("ktd", j)], w=[("KTb", b)], dma=f"ktb{b}")
                    op("sp", lambda e: e.dma_start(out=Vb[b].rearrange("p s e -> p (s e)"), in_=v_d4[j, :, hd, :]),
                       r=[("vd", j)], w=VbK[b], dma=f"vb{b}")
                b = cur["b"]
                diag = (j == tt)
                q0c = kb * 128 if diag else 0
                lb_ = cnt["l"] % 2
                pb = cnt["l"] % 2
                cnt["l"] += 1
                op("pe", lambda e: e.matmul(ps[lb_][:, q0c:TT], lhsT=KTb[b][:, kb * 128:(kb + 1) * 128], rhs=act[:, 16 + m, q0c:TT], start=True, stop=True),
                   r=[("KTb", b), AK(16 + m)], w=[PK(lb_)])
                op("act", lambda e: e.activation(out=probs[pb][:, q0c:TT], in_=ps[lb_][:, q0c:TT], func=AF.Exp, scale=QSCALE),
                   r=[PK(lb_)], w=[("probs", pb)])
                if diag:
                    op("pool", lambda e: e.tensor_tensor(out=probs[pb][:, kb * 128:(kb + 1) * 128], in0=probs[pb][:, kb * 128:(kb + 1) * 128],
                                                         in1=causal[:], op=ALU.mult), r=[("probs", pb), "causal"], w=[("probs", pb)])
                return dict(b=b, pb=pb, diag=diag)

            deferred = []

            def emit_O(it, info):
                hd, c, j, kb = it
                b, pb, diag = info["b"], info["pb"], info["diag"]
                for qb in range(kb if diag else 0, 4):
                    first = (j == 0 and kb == 0)
                    last = (diag and kb == qb)
                    op("pe", lambda e, qb=qb, first=first, last=last: e.matmul(
                        ps[obank[qb]][:, 0:257], lhsT=probs[pb][:, qb * 128:(qb + 1) * 128], rhs=Vb[b][:, kb, :], start=first, stop=last, skip_group_check=True),
                       r=[("probs", pb)] + VbK[b], w=[PK(obank[qb])])
                if not (j == tt and kb == 3):
                    return
                for qb in range(4):
                    ob = obank[qb]
                    op("dve", lambda e, qb=qb, ob=ob: e.reciprocal(out=rden[:, qb:qb + 1], in_=ps[ob][:, 256:257]), r=[PK(ob)], w=[("rden", qb)])
                    if c == 0:
                        op("dve", lambda e, qb=qb, ob=ob: e.tensor_scalar(out=o1[:, qb, :], in0=ps[ob][:, 0:256], scalar1=rden[:, qb:qb + 1], scalar2=None, op0=ALU.mult),
                           r=[PK(ob), ("rden", qb)], w=[("o1", qb)])
                    else:
                        op("dve", lambda e, qb=qb: e.tensor_scalar(out=rden[:, qb:qb + 1], in0=rden[:, qb:qb + 1], scalar1=NEGLAM, scalar2=None, op0=ALU.mult),
                           r=[("rden", qb), "small"], w=[("rden", qb)])
                        op("dve", lambda e, qb=qb, ob=ob: e.scalar_tensor_tensor(out=o1[:, qb, :], in0=ps[ob][:, 0:256], scalar=rden[:, qb:qb + 1], in1=o1[:, qb, :],
                                                                                 op0=ALU.mult, op1=ALU.add), r=[PK(ob), ("rden", qb), ("o1", qb)], w=[("o1", qb)])
                if c == 0:
                    return
                O1K = [("o1", qb) for qb in range(4)]
                op("act", lambda e: e.activation(out=o2, in_=o1, func=AF.Square), r=O1K, w=[("o2",)])
                op("dve", lambda e: e.reduce_sum(out=ssq, in_=o2, axis=AX.X), r=[("o2",)], w=[("ssq",)])
                op("act", lambda e: e.activation(out=ssq, in_=ssq, func=AF.Ln, scale=1.0 / 256, bias=epsc[:, 0:1]), r=[("ssq",), "epsc"], w=[("ssq",)])
                op("act", lambda e: e.activation(out=ssq, in_=ssq, func=AF.Exp, scale=-0.5), r=[("ssq",)], w=[("ssq",)])
                for qb in range(4):
                    op("dve", lambda e, qb=qb: e.scalar_tensor_tensor(out=yb[:, qb, :], in0=o1[:, qb, :], scalar=ssq[:, qb:qb + 1], in1=sublnb[:],
                                                                       op0=ALU.mult, op1=ALU.mult), r=[("o1", qb), ("ssq",), "sublnb"], w=[("yb", qb)])

                def tail(hd=hd):
                    for qb in range(4):
                        for e2 in range(2):
                            op("pe", lambda e, qb=qb, e2=e2: e.transpose(p7b[:, e2, qb * 128:(qb + 1) * 128], yb[:, qb, e2 * 128:(e2 + 1) * 128], ident[:]),
                               r=[("yb", qb), "ident"], w=[PK(7)])
                    op("act", lambda e: e.activation(out=act[:, 2 * hd:2 * hd + 2, :], in_=p7b, func=AF.Copy), r=[PK(7)], w=[AK(2 * hd), AK(2 * hd + 1)])
                deferred.append([3, tail])

            prev = None
            for it in iters:
                info = emit_L(it)
                if prev is not None:
                    emit_O(*prev)
                prev = (it, info)
                for dfr in list(deferred):
                    dfr[0] -= 1
                    if dfr[0] <= 0:
                        deferred.remove(dfr)
                        dfr[1]()
            emit_O(*prev)
            for dfr in deferred:
                dfr[1]()

        IDX = {nm: i for i, (nm, t) in reversed(list(enumerate(WLIST)))}
        XK = [("x", c) for c in range(NCH)]
        done = False
        for tt in range(NTT):
            t0 = tt * TT
            base = tt * NT
            op("sp", lambda e, t0=t0: e.dma_start(out=xT[:], in_=xT_v[:, :, t0:t0 + TT]), w=XK, dma="x")
            stages = []
            if stop_after != "load":
                norm_fm(G_MIX0)
            nh = {"load": 0, "norm": 0, "h1": 1}.get(stop_after, 16)
            hgrn_all(base + IDX["ainA"], nh)
            if nh == 16:
                linear_fm(base + IDX["aout"], 8, 2, 16, src_act, evac_resid)
            if stop_after not in ("mix0", "load", "norm", "h1"):
                mlp(base + IDX["up0"], G_MLP0)
            if stop_after not in ("mix0", "mlp0", "load", "norm", "h1"):
                ple(base + IDX["pp0"], G_PLE0, 0, t0)
            if stop_after is None or stop_after in ("mix1", "mlp1"):
                kv_stage(base + IDX["wk"], tt)
                attn_stage(base + IDX["bq"], tt)
                linear_fm(base + IDX["bout"], 8, 2, 16, src_act, evac_resid)
            if stop_after is None or stop_after == "mlp1":
                mlp(base + IDX["up1"], G_MLP1)
            if stop_after is None:
                ple(base + IDX["pp1"], G_PLE1, 1, t0)
            op("sp", lambda e, t0=t0: e.dma_start(out=out_v[:, :, t0:t0 + TT], in_=xT[:]), r=XK, dma="out")

        run = S_.emit(sem)
        with nc.Block() as block:
            @block.sync
            def _(eng):
                run("sp", eng)
                for s, v in S_.final_counts[1].items():
                    eng.wait_ge(sem(s), v)

            @block.scalar
            def _(eng):
                run("act", eng)

            @block.vector
            def _(eng):
                run("dve", eng)

            @block.gpsimd
            def _(eng):
                run("pool", eng)

            @block.tensor
            def _(eng):
                run("pe", eng)
    nc._n_ops = len(S_.ops)
    return nc


def host_prep(inputs):
    inp = {k: np.asarray(v) for k, v in inputs.items()}
    wf = host_weight_tiles(inp)
    vecs = host_vecs(inp)
    return inp, wf, vecs


def kernel(**inputs):
    inp, wf, vecs = host_prep(inputs)
    x = inp["x"]
    p = inp["p"]
    B, S, _ = x.shape
    nc = build_nc(S)
    in_maps = []
    for b in range(B):
        in_maps.append({
            "xT": np.ascontiguousarray(x[b].T),
            "pT": np.ascontiguousarray(p[:, b].transpose(0, 2, 1)),
            "wf": wf,
            "vecs": vecs,
        })
    res = run_bass_kernel_spmd(nc, in_maps, core_ids=list(range(B)))
    out = np.empty((B, S, D), np.float32)
    for b in range(B):
        out[b] = res.results[b]["outT"].T
    return out
```

```python
import math
from contextlib import ExitStack
import numpy as np
import concourse.bass as bass
import concourse.mybir as mybir
from concourse.bass_utils import run_bass_kernel_spmd

F32 = mybir.dt.float32
BF16 = mybir.dt.bfloat16
AF = mybir.ActivationFunctionType
ALU = mybir.AluOpType
AX = mybir.AxisListType

D = 2048
NCH = 16
TT = 512
DFF = 8192
EPS = 1e-6
WT = 4096
NSLOT = 6
CASTG = 4
LAM_INIT = 0.8 - 0.6 * math.exp(-0.3 * 1)
QSCALE = 1.0 / math.sqrt(128.0)

V_GAIN = 0
V_LB = 112
V_ONORM = 160
V_KN = 161
V_QN = 163
V_LAM = 165
V_SUBLN = 677
NV = 933
G_MIX0, G_MLP0, G_PLE0, G_KV, G_MIX1, G_MLP1, G_PLE1 = range(7)


class Sched:
    ENGS = ("pe", "act", "dve", "pool", "sp")

    def __init__(self):
        self.ops = []
        self.last_w = {}
        self.readers = {}

    def op(self, eng, fn, r=(), w=(), dma=None, waw_ok=False):
        i = len(self.ops)
        deps = set()
        for k in r:
            deps.update(self.last_w.get(k, ()))
        for k in w:
            rd = self.readers.get(k, [])
            lw = self.last_w.get(k, [])
            if rd:
                deps.update(rd)
                deps.update(lw)
                self.last_w[k] = [i]
                self.readers[k] = []
            elif waw_ok:
                self.last_w.setdefault(k, []).append(i)
            else:
                deps.update(lw)
                self.last_w[k] = [i]
        for k in r:
            if k not in w:
                self.readers.setdefault(k, []).append(i)
        deps.discard(i)
        self.ops.append(dict(eng=eng, fn=fn, deps=deps, dma=dma, isdma=dma is not None))
        return i

    def emit(self, sem_ctx):
        ops = self.ops
        n = len(ops)
        needed = [False] * n
        for o in ops:
            keep = set()
            best = {}
            for d in o["deps"]:
                od = ops[d]
                if od["isdma"]:
                    keep.add(d)
                    continue
                if od["eng"] == "pe" and o["eng"] == "pe" and not o["isdma"]:
                    continue
                if d > best.get(od["eng"], -1):
                    best[od["eng"]] = d
            keep.update(best.values())
            o["deps"] = keep
            for d in keep:
                needed[d] = True
        eng_cnt = {e: 0 for e in self.ENGS}
        dma_cnt = {}
        EPOCH = 4000
        DEPOCH = 200
        finals = {}
        for i, o in enumerate(ops):
            if o["isdma"]:
                s = o["dma"]
                c = dma_cnt.get(s, 0)
                dma_cnt[s] = c + 1
                name = f"dma:{s}:{c // DEPOCH}"
                o["sig"] = (name, (c % DEPOCH + 1) * 16)
                finals[name] = o["sig"][1]
            elif needed[i]:
                c = eng_cnt[o["eng"]]
                eng_cnt[o["eng"]] = c + 1
                o["sig"] = (f"eng:{o['eng']}:{c // EPOCH}", c % EPOCH + 1)
            else:
                o["sig"] = None
        self.final_counts = (eng_cnt, finals)
        streams = {e: [] for e in self.ENGS}
        for i, o in enumerate(ops):
            streams[o["eng"]].append(i)

        def run_stream(e, engobj):
            waited = {}
            for i in streams[e]:
                o = ops[i]
                need = {}
                for d in o["deps"]:
                    sname, val = ops[d]["sig"]
                    if val > need.get(sname, 0):
                        need[sname] = val
                for sname, val in need.items():
                    if waited.get(sname, 0) >= val:
                        continue
                    engobj.wait_ge(sem_ctx(sname), val)
                    waited[sname] = val
                ins = o["fn"](engobj)
                if o["sig"] is not None:
                    sname, val = o["sig"]
                    ins.then_inc(sem_ctx(sname), 16 if o["isdma"] else 1)

        return run_stream


def weight_tile_list():
    L = []
    for h in range(16):
        L.append(("ainA", h))
        L.append(("ainB", h))
    for t in range(8):
        L.append(("aout", t))
    for half in range(2):
        for t in range(16):
            L.append(("up0", half * 16 + t))
        for t in range(16):
            L.append(("dn0", half * 16 + t))
    L.append(("pp0", 0))
    for t in range(8):
        L.append(("pg0", t))
    for t in range(8):
        L.append(("wk", t))
    for t in range(8):
        L.append(("wv", t))
    for t in range(8):
        L.append(("bq", t))
    for t in range(8):
        L.append(("bout", t))
    for half in range(2):
        for t in range(16):
            L.append(("up1", half * 16 + t))
        for t in range(16):
            L.append(("dn1", half * 16 + t))
    L.append(("pp1", 0))
    for t in range(8):
        L.append(("pg1", t))
    return L


WLIST = weight_tile_list()
NT = len(WLIST)
NTP = ((NT + CASTG - 1) // CASTG) * CASTG


def _tile_kn(W, k0, kc, cols):
    sub = W[k0:k0 + kc * 128][:, cols]
    n = sub.shape[1]
    return np.ascontiguousarray(sub.reshape(kc, 128, n).transpose(1, 0, 2)).reshape(128, kc * n)


def host_weight_tiles(inp):
    out = np.zeros((NTP, 128, WT), np.float32)
    ar = np.arange
    for i, (nm, t) in enumerate(WLIST):
        if nm == "ainA":
            W = inp["a_w_in"][0]
            cols = np.concatenate([ar(t * 128, t * 128 + 128), ar(2048 + t * 128, 2048 + t * 128 + 128)])
            out[i] = _tile_kn(W, 0, 16, cols)
        elif nm == "ainB":
            W = inp["a_w_in"][0]
            cols = np.concatenate([ar(4096 + t * 128, 4096 + t * 128 + 128), ar(6144 + t * 128, 6144 + t * 128 + 128)])
            out[i] = _tile_kn(W, 0, 16, cols)
        elif nm in ("aout", "wk", "wv", "bq", "bout", "pg0", "pg1"):
            W = {"aout": inp["a_w_out"][0], "wk": inp["w_k"], "wv": inp["w_v"], "bq": inp["b_w_q"][0],
                 "bout": inp["b_w_out"][0], "pg0": inp["ple_gate"][0], "pg1": inp["ple_gate"][1]}[nm]
            out[i] = _tile_kn(W, 0, 16, ar(t * 256, t * 256 + 256))
        elif nm in ("up0", "up1"):
            W = inp["mlp_up"][int(nm[2])]
            out[i] = _tile_kn(W, 0, 16, ar(t * 256, t * 256 + 256))
        elif nm in ("dn0", "dn1"):
            W = inp["mlp_down"][int(nm[2])]
            half, m = divmod(t, 16)
            out[i] = _tile_kn(W, half * 4096, 32, ar(m * 128, m * 128 + 128))
        elif nm in ("pp0", "pp1"):
            W = inp["ple_proj"][int(nm[2])]
            out[i] = _tile_kn(W, 0, 2, ar(0, 2048))
    return out


def host_vecs(inp):
    v = np.zeros((128, NV), np.float32)
    gains = [inp["ln_mix"][0], inp["ln_mlp"][0], inp["ln_ple"][0], inp["kv_norm"],
             inp["ln_mix"][1], inp["ln_mlp"][1], inp["ln_ple"][1]]
    for i, g in enumerate(gains):
        v[:, V_GAIN + i * 16:V_GAIN + (i + 1) * 16] = g.reshape(16, 128).T
    for r in range(3):
        v[:, V_LB + r * 16:V_LB + (r + 1) * 16] = inp["a_lb"][r].reshape(16, 128).T
    v[:, V_ONORM] = inp["a_onorm"][0]
    v[:, V_KN] = inp["k_norm"][0]
    v[:, V_KN + 1] = inp["k_norm"][1]
    v[:, V_QN] = inp["q_norm"][0, 0]
    v[:, V_QN + 1] = inp["q_norm"][0, 1]
    lam = np.concatenate([inp["lam_q1"][0], inp["lam_k1"][0], inp["lam_q2"][0], inp["lam_k2"][0]])
    v[:, V_LAM:V_LAM + 512] = lam[None, :]
    v[:, V_SUBLN:V_SUBLN + 256] = inp["b_subln"][0][None, :]
    return v


def build_nc(S, stop_after=None):
    import os
    HSTOP = int(os.environ.get('DBG_HSTOP', 99))
    NTT = S // TT
    nc = bass.Bass("TRN2", target_bir_lowering=False)
    xT_d = nc.dram_tensor("xT", [D, S], F32, kind="ExternalInput").ap()
    pT_d = nc.dram_tensor("pT", [2, 256, S], F32, kind="ExternalInput").ap()
    wf_d = nc.dram_tensor("wf", [NTP, 128, WT], F32, kind="ExternalInput").ap()
    vec_d = nc.dram_tensor("vecs", [128, NV], F32, kind="ExternalInput").ap()
    out_d = nc.dram_tensor("outT", [D, S], F32, kind="ExternalOutput").ap()
    wb_d = nc.dram_tensor("wb", [NTP, 128, WT], BF16, kind="Internal").ap()
    kt_d = nc.dram_tensor("ktc", [NTT, 128, 16 * TT], BF16, kind="Internal").ap()
    v_d = nc.dram_tensor("vc", [NTT, 128, 8 * 4 * 257], BF16, kind="Internal").ap()

    xT_v = xT_d.rearrange("(c p) t -> p c t", p=128)
    out_v = out_d.rearrange("(c p) t -> p c t", p=128)

    S_ = Sched()
    op = S_.op
    es = ExitStack()
    with es:
        def sb(name, shape, dt):
            return es.enter_context(nc.sbuf_tensor(name, shape, dt))

        xT = sb("xTs", [128, NCH, TT], F32)
        hT = sb("hT", [128, NCH, TT], BF16)
        sqb = sb("sqb", [128, 4, TT], BF16)
        rstd = sb("rstd", [128, TT], F32)
        wr = sb("wring", [128, NSLOT, WT], BF16)
        wpj = sb("wpj", [128, WT], BF16)
        act = sb("actT", [128, 32, TT], BF16)
        U = sb("U", [128, 20, TT], F32)
        Scar = sb("Scar", [128, 16, 128], F32)
        vecs = sb("vecs_s", [128, NV], F32)
        pTb = sb("pTb", [128, 2, TT], BF16)
        pTf = sb("pTf", [128, 2, TT], F32)
        ident = sb("ident", [128, 128], BF16)
        ones = sb("ones", [128, 128], BF16)
        causal = sb("causal", [128, 128], BF16)
        maskbd = sb("maskbd", [128, TT], F32)
        scanm = sb("scanm", [128, TT], F32)
        small = sb("small", [128, 64], F32)
        epsc = sb("epsc", [128, 1], F32)
        onec = sb("onec", [128, 1], F32)
        hmask = sb("hmask", [128, 2], F32)
        lbt = sb("lbt", [128, 48], F32)
        lamt = sb("lamt", [128, 256], F32)
        sublnb = sb("sublnb", [128, 256], F32)
        ps = [es.enter_context(nc.psum_tensor(f"ps{i}", [128, 512], F32)) for i in range(8)]

        sems = {}

        def sem(name):
            if name not in sems:
                sems[name] = es.enter_context(nc.semaphore(name.replace(":", "_")))
            return sems[name]

        def Uf(i, n=1):
            if n == 1:
                return U[:, i, :]
            return U[:, i:i + n, :]

        def Ub(i):
            return U[:, i, :].bitcast(BF16)

        UK = lambda i: ("U", i)
        PK = lambda i: ("ps", i)
        AK = lambda i: ("act", i)

        op("sp", lambda e: e.dma_start(out=vecs[:], in_=vec_d), w=["vecs"], dma="vec")
        actf = act[:].rearrange("p a t -> p (a t)").bitcast(F32)
        Uflat = U[:].rearrange("p a t -> p (a t)")
        stages = [(Uflat[:, 0:WT], [("U", i) for i in range(0, 8)]),
                  (Uflat[:, WT:2 * WT], [("U", i) for i in range(8, 16)]),
                  (actf[:, 0:WT], [("act", i) for i in range(0, 16)]),
                  (actf[:, WT:2 * WT], [("act", i) for i in range(16, 32)])]
        import os
        ncast = int(os.environ.get("DBG_NCAST", NT))
        def cast_ld(i):
            stg, skeys = stages[i % 4]
            op("sp", lambda e: e.dma_start(out=stg, in_=wf_d[i]), w=skeys, dma=f"cl{i % 4}")
        for i in range(min(4, ncast)):
            cast_ld(i)
        for i in range(ncast):
            stg, skeys = stages[i % 4]
            slot = i % NSLOT
            if i % 2 == 0:
                op("act", lambda e, stg=stg, slot=slot: e.activation(out=wr[:, slot, :], in_=stg, func=AF.Copy), r=skeys, w=[("w", slot)])
            else:
                op("dve", lambda e, stg=stg, slot=slot: e.tensor_copy(out=wr[:, slot, :], in_=stg), r=skeys, w=[("w", slot)])
            op("sp", lambda e, i=i, slot=slot: e.dma_start(out=wb_d[i], in_=wr[:, slot, :]), r=[("w", slot)], w=[("wb", i)], dma=f"cs{slot}")
            if i + 4 < ncast:
                cast_ld(i + 4)
        op("dve", lambda e: e.memset(ones[:], 1.0), w=["ones"])
        op("dve", lambda e: e.memset(epsc[:], EPS), w=["epsc"])
        op("dve", lambda e: e.memset(onec[:], 1.0), w=["onec"])
        op("dve", lambda e: e.memset(ident[:], 1.0), w=["ident"])
        op("pool", lambda e: e.affine_select(out=ident[:], in_=ident[:], pattern=[[-1, 128]], compare_op=ALU.is_equal,
                                              fill=0.0, base=0, channel_multiplier=1), r=["ident"], w=["ident"])
        op("dve", lambda e: e.memset(causal[:], 1.0), w=["causal"])
        op("pool", lambda e: e.affine_select(out=causal[:], in_=causal[:], pattern=[[1, 128]], compare_op=ALU.is_ge,
                                              fill=0.0, base=0, channel_multiplier=-1), r=["causal"], w=["causal"])
        op("dve", lambda e: e.memset(maskbd[:], 1.0), w=["maskbd"])
        mb3 = maskbd[:].rearrange("p (j t) -> p j t", t=128)
        op("pool", lambda e: e.affine_select(out=mb3, in_=mb3, pattern=[[0, 4], [1, 128]], compare_op=ALU.is_ge,
                                              fill=0.0, base=0, channel_multiplier=-1), r=["maskbd"], w=["maskbd"])
        op("dve", lambda e: e.memset(scanm[:], 1.0), w=["scanm"])
        sm3 = scanm[:].rearrange("p (c t) -> p c t", t=128)
        op("dve", lambda e: e.memset(sm3[:, :, 0:1], 0.0), r=["scanm"], w=["scanm"])
        op("dve", lambda e: e.memset(Scar[:], 0.0), w=["Scar"])
        op("act", lambda e: e.activation(out=lbt[:], in_=vecs[:, V_LB:V_LB + 48], func=AF.Exp), r=["vecs"], w=["lbt"])
        op("dve", lambda e: e.tensor_tensor(out=small[:, 32:48], in0=lbt[:, 0:16], in1=lbt[:, 16:32], op=ALU.add), r=["lbt"], w=["small"])
        op("dve", lambda e: e.tensor_tensor(out=small[:, 32:48], in0=small[:, 32:48], in1=lbt[:, 32:48], op=ALU.add), r=["lbt", "small"], w=["small"])
        op("dve", lambda e: e.reciprocal(out=small[:, 32:48], in_=small[:, 32:48]), r=["small"], w=["small"])
        op("dve", lambda e: e.tensor_tensor(out=small[:, 0:16], in0=lbt[:, 0:16], in1=small[:, 32:48], op=ALU.mult), r=["lbt", "small"], w=["small"])
        op("dve", lambda e: e.tensor_scalar(out=small[:, 16:32], in0=small[:, 0:16], scalar1=-1.0, scalar2=1.0, op0=ALU.mult, op1=ALU.add), r=["small"], w=["small"])
        op("dve", lambda e: e.tensor_tensor(out=lamt[:, 0:128], in0=vecs[:, V_LAM:V_LAM + 128], in1=vecs[:, V_LAM + 128:V_LAM + 256], op=ALU.mult), r=["vecs"], w=["lamt"])
        op("dve", lambda e: e.tensor_tensor(out=lamt[:, 128:256], in0=vecs[:, V_LAM + 256:V_LAM + 384], in1=vecs[:, V_LAM + 384:V_LAM + 512], op=ALU.mult), r=["vecs", "lamt"], w=["lamt"])
        op("dve", lambda e: e.reduce_sum(out=small[:, 49:51], in_=lamt[:].rearrange("p (a b) -> p a b", b=128), axis=AX.X), r=["lamt", "small"], w=["small"])
        op("act", lambda e: e.activation(out=small[:, 49:51], in_=small[:, 49:51], func=AF.Exp), r=["small"], w=["small"])
        op("dve", lambda e: e.tensor_tensor(out=small[:, 48:49], in0=small[:, 50:51], in1=small[:, 49:50], op=ALU.subtract), r=["small"], w=["small"])
        op("dve", lambda e: e.tensor_scalar(out=small[:, 48:49], in0=small[:, 48:49], scalar1=-LAM_INIT, scalar2=None, op0=ALU.add), r=["small"], w=["small"])
        op("dve", lambda e: e.tensor_scalar(out=sublnb[:], in0=vecs[:, V_SUBLN:V_SUBLN + 256], scalar1=1.0 - LAM_INIT, scalar2=None, op0=ALU.mult), r=["vecs"], w=["sublnb"])
        LB = lambda h: small[:, h:h + 1]
        OML = lambda h: small[:, 16 + h:17 + h]
        NEGLAM = small[:, 48:49]

        wstate = dict(issued=0)

        def w_issue_upto(n):
            while wstate["issued"] <= n:
                q = wstate["issued"]
                gi = q % NT
                if WLIST[gi][0] in ("pp0", "pp1"):
                    wstate["issued"] += 1
                    continue
                slot = q % NSLOT
                op("sp", lambda e, gi=gi, slot=slot: e.dma_start(out=wr[:, slot, :], in_=wb_d[gi]),
                   r=[("wb", gi)], w=[("w", slot)], dma=f"w{slot}")
                wstate["issued"] += 1

        def w_get(q):
            w_issue_upto(min(q + NSLOT - 1, NTT * NT - 1))
            return q % NSLOT

        ctr = dict(bank=0, alt=0)

        def alt(a, b):
            ctr["alt"] += 1
            return a if ctr["alt"] % 2 else b

        def linear_fm(q0, ntiles, mper, kc, src, evac, banks=(0, 1, 2, 3)):
            ncols = WT // kc
            for ti in range(ntiles):
                slot = w_get(q0 + ti)
                for j in range(mper):
                    m = ti * mper + j
                    bank = banks[ctr["bank"] % len(banks)]
                    ctr["bank"] += 1
                    for k in range(kc):
                        rhs, key = src(k)
                        op("pe", lambda e, bank=bank, slot=slot, k=k, j=j, rhs=rhs: e.matmul(
                            ps[bank][:], lhsT=wr[:, slot, k * ncols + j * 128:k * ncols + (j + 1) * 128], rhs=rhs,
                            start=(k == 0), stop=(k == kc - 1)), r=[("w", slot), key], w=[PK(bank)])
                    evac(m, bank)

        def norm_fm(gidx):
            for c in range(NCH):
                s = c % 4
                if c % 2 == 0:
                    op("act", lambda e, c=c, s=s: e.activation(out=sqb[:, s, :], in_=xT[:, c, :], func=AF.Square),
                       r=[("x", c)], w=[("sqb", s)])
                else:
                    op("pool", lambda e, c=c, s=s: e.tensor_tensor(out=sqb[:, s, :], in0=xT[:, c, :], in1=xT[:, c, :], op=ALU.mult),
                       r=[("x", c)], w=[("sqb", s)])
                op("pe", lambda e, c=c, s=s: e.matmul(ps[4][:], lhsT=ones[:], rhs=sqb[:, s, :], start=(c == 0), stop=(c == NCH - 1)),
                   r=["ones", ("sqb", s)], w=[PK(4)])
            op("act", lambda e: e.activation(out=rstd[:], in_=ps[4][:], func=AF.Ln, scale=1.0 / D, bias=epsc[:, 0:1]), r=[PK(4), "epsc"], w=["rstd"])
            op("act", lambda e: e.activation(out=rstd[:], in_=rstd[:], func=AF.Exp, scale=-0.5), r=["rstd"], w=["rstd"])
            for c in range(NCH):
                col = V_GAIN + gidx * 16 + c
                op("dve", lambda e, c=c, col=col: e.scalar_tensor_tensor(
                    out=hT[:, c, :], in0=xT[:, c, :], scalar=vecs[:, col:col + 1], in1=rstd[:], op0=ALU.mult, op1=ALU.mult),
                   r=[("x", c), "vecs", "rstd"], w=[("h", c)])

        def src_h(k):
            return hT[:, k, :], ("h", k)

        def src_act(k):
            return act[:, k, :], AK(k)

        def evac_resid(m, bank):
            op("dve", lambda e: e.tensor_tensor(out=xT[:, m, :], in0=xT[:, m, :], in1=ps[bank][:], op=ALU.add),
               r=[PK(bank), ("x", m)], w=[("x", m)])

        def hgrn_pieces(q0, h):
            par = h % 2
            f_, k_, lg_, q_, _, e1_, dm_, eq_, o_, G_ = [Uf(i) for i in range(10)]
            sg_ = Uf(4) if par == 0 else Uf(17)
            SGK = UK(4) if par == 0 else UK(17)
            qt = Ub(10)[:, 0:TT]
            kt = Ub(10)[:, TT:2 * TT]
            khat = Ub(11)[:, 0:TT]
            khatT = Ub(11)[:, TT:2 * TT]
            scm = Ub(12)[:, 0:TT]
            vtok = Ub(12)[:, TT:2 * TT] if par == 0 else Ub(14)[:, TT:2 * TT]
            VK = ("vtok", par)
            osq = Ub(13)[:, 0:TT]
            qS = Ub(13)[:, TT:2 * TT]
            Sb = Ub(14)[:, 0:TT]
            Sall = U[:, 15:17, :].rearrange("p a t -> p (a t)")
            egl = small[:, 52:56]
            e3_ = lg_
            st = {}

            def mm_fm(bank, slot, colofs):
                for k in range(NCH):
                    op("pe", lambda e, k=k: e.matmul(ps[bank][:], lhsT=wr[:, slot, k * 256 + colofs:k * 256 + colofs + 128],
                                                     rhs=hT[:, k, :], start=(k == 0), stop=(k == NCH - 1)),
                       r=[("w", slot), ("h", k)], w=[PK(bank)])

            def Pq():
                st["sA"] = w_get(q0 + 2 * h)
                st["sB"] = (q0 + 2 * h + 1) % NSLOT
                mm_fm(0, st["sA"], 0)
                op("act", lambda e: e.activation(out=q_, in_=ps[0][:], func=AF.Exp, scale=-1.0), r=[PK(0)], w=[UK(3)])
                op("act", lambda e: e.activation(out=q_, in_=q_, func=AF.Ln, bias=onec[:, 0:1]), r=[UK(3), "onec"], w=[UK(3)])
                op("act", lambda e: e.activation(out=q_, in_=q_, func=AF.Exp, scale=-1.0), r=[UK(3)], w=[UK(3)])
                op("dve", lambda e: e.tensor_tensor(out=q_, in0=q_, in1=ps[0][:], op=ALU.mult), r=[UK(3), PK(0)], w=[UK(3)])

            def Pf():
                mm_fm(1, st["sA"], 128)
                op("act", lambda e: e.activation(out=f_, in_=ps[1][:], func=AF.Exp, scale=-1.0), r=[PK(1)], w=[UK(0)])
                op("act", lambda e: e.activation(out=f_, in_=f_, func=AF.Ln, bias=onec[:, 0:1]), r=[UK(0), "onec"], w=[UK(0)])
                op("act", lambda e: e.activation(out=f_, in_=f_, func=AF.Exp, scale=-1.0), r=[UK(0)], w=[UK(0)])

            def Pv():
                sB = st["sB"]
                for sub in range(4):
                    for k in range(NCH):
                        op("pe", lambda e, k=k, sub=sub: e.matmul(ps[3][:, sub * 128:(sub + 1) * 128], lhsT=hT[:, k, sub * 128:(sub + 1) * 128],
                                                                  rhs=wr[:, sB, k * 256:k * 256 + 128], start=(k == 0), stop=(k == NCH - 1)),
                           r=[("w", sB), ("h", k)], w=[PK(3)])
                op("act", lambda e: e.activation(out=vtok, in_=ps[3][:], func=AF.Copy), r=[PK(3)], w=[VK])

            def Pg():
                mm_fm(1, st["sB"], 128)
                op("act", lambda e: e.activation(out=sg_, in_=ps[1][:], func=AF.Exp, scale=-1.0), r=[PK(1)], w=[SGK])
                op("act", lambda e: e.activation(out=sg_, in_=sg_, func=AF.Ln, bias=onec[:, 0:1]), r=[SGK, "onec"], w=[SGK])
                op("act", lambda e: e.activation(out=sg_, in_=sg_, func=AF.Exp, scale=-1.0), r=[SGK], w=[SGK])
                op("dve", lambda e: e.tensor_tensor(out=sg_, in0=sg_, in1=ps[1][:], op=ALU.mult), r=[SGK, PK(1)], w=[SGK])

            def RA():
                op("pool", lambda e: e.tensor_scalar(out=f_, in0=f_, scalar1=OML(h), scalar2=LB(h), op0=ALU.mult, op1=ALU.add),
                   r=[UK(0), "small"], w=[UK(0)])
                op("pool", lambda e: e.tensor_scalar(out=k_, in0=f_, scalar1=-1.0, scalar2=1.0, op0=ALU.mult, op1=ALU.add),
                   r=[UK(0)], w=[UK(1)])
                op("act", lambda e: e.activation(out=lg_, in_=f_, func=AF.Ln), r=[UK(0)], w=[UK(2)])
                op("dve", lambda e: e.tensor_tensor_scan(out=G_, data0=scanm[:], data1=lg_, initial=0.0, op0=ALU.mult, op1=ALU.add),
                   r=["scanm", UK(2)], w=[UK(9)])
                G3 = G_.rearrange("p (c t) -> p c t", t=128)
                op("act", lambda e: e.activation(out=e1_, in_=G_, func=AF.Exp), r=[UK(9)], w=[UK(5)])
                dm3 = dm_.rearrange("p (c t) -> p c t", t=128)
                op("pool", lambda e: e.tensor_tensor(out=dm3, in0=G3, in1=G3[:, :, 63:64].to_broadcast([128, 4, 128]), op=ALU.subtract),
                   r=[UK(9)], w=[UK(6)])
                op("act", lambda e: e.activation(out=eq_, in_=dm_, func=AF.Exp), r=[UK(6)], w=[UK(7)])
                op("act", lambda e: e.activation(out=dm_, in_=dm_, func=AF.Exp, scale=-1.0), r=[UK(6), UK(7)], w=[UK(6)])
                e33 = e3_.rearrange("p (c t) -> p c t", t=128)
                op("pool", lambda e: e.tensor_tensor(out=e33, in0=G3[:, :, 127:128].to_broadcast([128, 4, 128]), in1=G3, op=ALU.subtract),
                   r=[UK(9), UK(2)], w=[UK(2)])
                op("act", lambda e: e.activation(out=e3_, in_=e3_, func=AF.Exp), r=[UK(2)], w=[UK(2)])
                op("act", lambda e: e.activation(out=egl.unsqueeze(2), in_=G3[:, :, 127:128], func=AF.Exp), r=[UK(9)], w=["egl"])
                op("dve", lambda e: e.tensor_tensor(out=qS, in0=q_, in1=e1_, op=ALU.mult), r=[UK(3), UK(5)], w=[("qS",)])
                op("dve", lambda e: e.tensor_tensor(out=qt, in0=q_, in1=eq_, op=ALU.mult), r=[UK(3), UK(7)], w=[("qt",)])
                op("pool", lambda e: e.tensor_tensor(out=kt, in0=k_, in1=dm_, op=ALU.mult), r=[UK(1), UK(6)], w=[("kt",)])
                op("pool", lambda e: e.tensor_tensor(out=khat, in0=k_, in1=e3_, op=ALU.mult), r=[UK(1), UK(2)], w=[("khat",)])

            def RB():
                p5b = ps[5][:].bitcast(BF16)
                for j in range(4):
                    op("pe", lambda e, j=j: e.transpose(p5b[:, j * 128:(j + 1) * 128], khat[:, j * 128:(j + 1) * 128], ident[:]),
                       r=[("khat",), "ident"], w=[PK(5)])
                op("act", lambda e: e.activation(out=khatT, in_=p5b[:, 0:TT], func=AF.Copy), r=[PK(5)], w=[("khatT",)])

            def RC():
                for j in range(4):
                    op("pe", lambda e, j=j: e.matmul(ps[6][:, j * 128:(j + 1) * 128], lhsT=kt[:, j * 128:(j + 1) * 128],
                                                     rhs=qt[:, j * 128:(j + 1) * 128], start=True, stop=True),
                       r=[("kt",), ("qt",)], w=[PK(6)])
                op("dve", lambda e: e.tensor_tensor(out=scm, in0=ps[6][:], in1=maskbd[:], op=ALU.mult), r=[PK(6), "maskbd"], w=[("scm",)])

            def RD():
                for j in range(4):
                    op("pe", lambda e, j=j: e.matmul(ps[2][:, j * 128:(j + 1) * 128], lhsT=khatT[:, j * 128:(j + 1) * 128],
                                                     rhs=vtok[:, j * 128:(j + 1) * 128], start=True, stop=True),
                       r=[("khatT",), VK], w=[PK(2)])
                op("pool", lambda e: e.tensor_copy(out=Sall[:, 0:128], in_=Scar[:, h, :]), r=[("Scar", h)], w=[("Sall", 0)])
                for c in range(4):
                    op("dve", lambda e, c=c: e.scalar_tensor_tensor(
                        out=Sall[:, (c + 1) * 128:(c + 2) * 128], in0=Sall[:, c * 128:(c + 1) * 128], scalar=egl[:, c:c + 1],
                        in1=ps[2][:, c * 128:(c + 1) * 128], op0=ALU.mult, op1=ALU.add),
                       r=[("Sall", c), "egl", PK(2)], w=[("Sall", c + 1)])
                op("pool", lambda e: e.tensor_copy(out=Scar[:, h, :], in_=Sall[:, 512:640]), r=[("Sall", 4)], w=[("Scar", h)])
                op("act", lambda e: e.activation(out=Sb, in_=Sall[:, 0:512], func=AF.Copy), r=[("Sall", c) for c in range(4)], w=[("Sb",)])

            def RE():
                for j in range(4):
                    op("pe", lambda e, j=j: e.matmul(ps[7][:, j * 128:(j + 1) * 128], lhsT=vtok[:, j * 128:(j + 1) * 128],
                                                     rhs=scm[:, j * 128:(j + 1) * 128], start=True, stop=False),
                       r=[VK, ("scm",)], w=[PK(7)])
                    op("pe", lambda e, j=j: e.matmul(ps[7][:, j * 128:(j + 1) * 128], lhsT=Sb[:, j * 128:(j + 1) * 128],
                                                     rhs=qS[:, j * 128:(j + 1) * 128], start=False, stop=True),
                       r=[("Sb",), ("qS",)], w=[PK(7)])
                op("dve", lambda e: e.tensor_copy(out=o_, in_=ps[7][:]), r=[PK(7)], w=[UK(8)])
                op("act", lambda e: e.activation(out=osq, in_=o_, func=AF.Square), r=[UK(8)], w=[("osq",)])
                op("pe", lambda e: e.matmul(ps[4][:], lhsT=ones[:], rhs=osq, start=True, stop=True), r=["ones", ("osq",)], w=[PK(4)])
                op("act", lambda e: e.activation(out=G_, in_=ps[4][:], func=AF.Ln, scale=1.0 / 128, bias=epsc[:, 0:1]), r=[PK(4), "epsc"], w=[UK(9)])
                op("act", lambda e: e.activation(out=G_, in_=G_, func=AF.Exp, scale=-0.5), r=[UK(9)], w=[UK(9)])
                op("dve", lambda e: e.scalar_tensor_tensor(out=o_, in0=o_, scalar=vecs[:, V_ONORM:V_ONORM + 1], in1=G_, op0=ALU.mult, op1=ALU.mult),
                   r=[UK(8), UK(9), "vecs"], w=[UK(8)])
                op("pool", lambda e: e.tensor_tensor(out=act[:, h, :], in0=o_, in1=sg_, op=ALU.mult), r=[UK(8), SGK], w=[AK(h)])

            return (Pq, Pf, Pv, Pg), (RA, RB, RC, RD, RE)

        def hgrn_all(q0, nh):
            pcs = [hgrn_pieces(q0, h) for h in range(nh)]
            for h in range(nh):
                P, R = pcs[h]
                if h == 0:
                    for p in P:
                        p()
                R[0]()
                if h + 1 < nh:
                    Pn = pcs[h + 1][0]
                    Pn[0](); R[1](); Pn[1](); R[2](); Pn[2](); R[3](); Pn[3](); R[4]()
                else:
                    R[1](); R[2](); R[3](); R[4]()

        def mlp(q0, gidx):
            norm_fm(gidx)
            for half in range(2):
                def evac_up(m, bank):
                    t = m % 2
                    op("act", lambda e: e.activation(out=Uf(t), in_=ps[bank][:], func=AF.Relu), r=[PK(bank)], w=[UK(t)])
                    op("pool", lambda e: e.tensor_tensor(out=act[:, m, :], in0=Uf(t), in1=Uf(t), op=ALU.mult), r=[UK(t)], w=[AK(m)])
                linear_fm(q0 + half * 32, 16, 2, 16, src_h, evac_up)
                linear_fm(q0 + half * 32 + 16, 16, 1, 32, src_act, evac_resid)

        def ple(q0, gidx, layer, t0):
            gpp = q0 % NT
            op("sp", lambda e: e.dma_start(out=wpj[:], in_=wb_d[gpp]), r=[("wb", gpp)], w=["wpj"], dma="wpj")
            op("sp", lambda e: e.dma_start(out=pTf[:], in_=pT_d[layer].rearrange("(k p) t -> p k t", p=128)[:, :, t0:t0 + TT]),
               w=["pTf"], dma="pt")
            op("pool", lambda e: e.tensor_copy(out=pTb[:], in_=pTf[:]), r=["pTf"], w=["pTb"])
            norm_fm(gidx)

            def evac_gate(m, bank):
                t = 2 + m % 2
                op("act", lambda e: e.activation(out=Uf(t), in_=ps[bank][:], func=AF.Exp, scale=-1.0), r=[PK(bank)], w=[UK(t)])
                op("act", lambda e: e.activation(out=Uf(t), in_=Uf(t), func=AF.Ln, bias=onec[:, 0:1]), r=[UK(t), "onec"], w=[UK(t)])
                op("act", lambda e: e.activation(out=Uf(t), in_=Uf(t), func=AF.Exp, scale=-1.0), r=[UK(t)], w=[UK(t)])
                pb = 5 + m % 2
                for k in range(2):
                    op("pe", lambda e, k=k: e.matmul(ps[pb][:], lhsT=wpj[:, k * 2048 + m * 128:k * 2048 + (m + 1) * 128], rhs=pTb[:, k, :],
                                                     start=(k == 0), stop=(k == 1)), r=["wpj", "pTb"], w=[PK(pb)])
                op("dve", lambda e: e.tensor_tensor(out=Uf(t), in0=Uf(t), in1=ps[pb][:], op=ALU.mult), r=[UK(t), PK(pb)], w=[UK(t)])
                op("pool", lambda e: e.tensor_tensor(out=xT[:, m, :], in0=xT[:, m, :], in1=Uf(t), op=ALU.add), r=[UK(t), ("x", m)], w=[("x", m)])
            linear_fm(q0 + 1, 8, 2, 16, src_h, evac_gate)

        def headnorm_evac(bank, dst, dkey, ncol, t):
            sq = sqb[:, t, :]
            op("dve", lambda e: e.tensor_copy(out=Uf(18 + t), in_=ps[bank][:]), r=[PK(bank)], w=[UK(18 + t)])
            op("act", lambda e: e.activation(out=sq, in_=Uf(18 + t), func=AF.Square), r=[UK(18 + t)], w=[("sqb", t)])
            op("pe", lambda e: e.matmul(ps[4][:], lhsT=ones[:], rhs=sq, start=True, stop=True), r=["ones", ("sqb", t)], w=[PK(4)])
            op("act", lambda e: e.activation(out=rstd[:], in_=ps[4][:], func=AF.Ln, scale=1.0 / 128, bias=epsc[:, 0:1]), r=[PK(4), "epsc"], w=["rstd"])
            op("act", lambda e: e.activation(out=rstd[:], in_=rstd[:], func=AF.Exp, scale=-0.5), r=["rstd"], w=["rstd"])
            op("dve", lambda e: e.scalar_tensor_tensor(out=dst, in0=Uf(18 + t), scalar=vecs[:, ncol:ncol + 1], in1=rstd[:], op0=ALU.mult, op1=ALU.mult),
               r=[UK(18 + t), "rstd", "vecs"], w=[dkey])

        VSTK = [UK(i) for i in range(9)] + [("KTb", 0), ("KTb", 1)]

        def kv_stage(q0, tt):
            norm_fm(G_KV)

            def evac_k(m, bank):
                headnorm_evac(bank, act[:, m, :], AK(m), V_KN + (m % 2), m % 2)
            linear_fm(q0, 8, 2, 16, src_h, evac_k, banks=(0, 1))
            op("sp", lambda e: e.dma_start(out=kt_d[tt], in_=act[:, 0:16, :].rearrange("p a t -> p (a t)")),
               r=[AK(m) for m in range(16)], w=[("ktd", tt)], dma="kts")
            Vst = U[:, 0:9, :].rearrange("p a t -> p (a t)").bitcast(BF16)[:, 0:8 * 4 * 257].rearrange("p (h s e) -> p h s e", h=8, s=4)
            op("pool", lambda e: e.memset(Vst[:, :, :, 256:257], 1.0), w=VSTK)
            for hd in range(8):
                slot = w_get(q0 + 8 + hd)
                for sub in range(4):
                    bank = (0, 1, 2, 3)[ctr["bank"] % 4]
                    ctr["bank"] += 1
                    for k in range(NCH):
                        op("pe", lambda e, k=k, sub=sub, bank=bank, slot=slot: e.matmul(
                            ps[bank][:, 0:256], lhsT=hT[:, k, sub * 128:(sub + 1) * 128], rhs=wr[:, slot, k * 256:(k + 1) * 256],
                            start=(k == 0), stop=(k == NCH - 1)), r=[("w", slot), ("h", k)], w=[PK(bank)])
                    if ctr["bank"] % 2:
                        op("dve", lambda e, hd=hd, sub=sub, bank=bank: e.tensor_copy(out=Vst[:, hd, sub, 0:256], in_=ps[bank][:, 0:256]),
                           r=[PK(bank)], w=VSTK, waw_ok=True)
                    else:
                        op("act", lambda e, hd=hd, sub=sub, bank=bank: e.activation(out=Vst[:, hd, sub, 0:256], in_=ps[bank][:, 0:256], func=AF.Copy),
                           r=[PK(bank)], w=VSTK, waw_ok=True)
            op("sp", lambda e: e.dma_start(out=v_d[tt], in_=U[:, 0:9, :].rearrange("p a t -> p (a t)").bitcast(BF16)[:, 0:8 * 4 * 257]),
               r=VSTK, w=[("vd", tt)], dma="vs")

        def attn_stage(q0, tt):
            norm_fm(G_MIX1)

            def evac_q(m, bank):
                headnorm_evac(bank, act[:, 16 + m, :], AK(16 + m), V_QN + (m % 2), m % 2)
            linear_fm(q0, 8, 2, 16, src_h, evac_q, banks=(0, 1))
            KTb = [Ub(0)[:, 0:TT], Ub(0)[:, TT:2 * TT]]
            Vb = [U[:, 2:4, :].rearrange("p a t -> p (a t)").bitcast(BF16)[:, 0:4 * 257].rearrange("p (s e) -> p s e", s=4),
                  U[:, 4:6, :].rearrange("p a t -> p (a t)").bitcast(BF16)[:, 0:4 * 257].rearrange("p (s e) -> p s e", s=4)]
            VbK = [[UK(2), UK(3)], [UK(4), UK(5)]]
            probs = [Ub(6)[:, 0:TT], Ub(6)[:, TT:2 * TT]]
            o1 = U[:, 7:9, :].rearrange("p a t -> p (a t)").rearrange("p (q e) -> p q e", q=4)
            o2 = U[:, 9:11, :].rearrange("p a t -> p (a t)").rearrange("p (q e) -> p q e", q=4)
            yb = Ub(11).rearrange("p (q e) -> p q e", q=4)
            rden = small[:, 60:64]
            ssq = small[:, 40:44]
            obank = (2, 3, 5, 6)
            kv_d4 = kt_d.rearrange("n p (m t) -> n p m t", m=16)
            v_d4 = v_d.rearrange("n p (h x) -> n p h x", h=8)
            cnt = dict(b=0, l=0)
            p7b = ps[7][:].bitcast(BF16).rearrange("p (a t) -> p a t", a=2)
            iters = [(hd, c, j, kb) for hd in range(8) for c in range(2) for j in range(tt + 1) for kb in range(4)]
            cur = dict(b=0)

            def emit_L(it):
                hd, c, j, kb = it
                m = 2 * hd + c
                if kb == 0:
                    b = cnt["b"] % 2
                    cnt["b"] += 1
                    cur["b"] = b
                    op("sp", lambda e: e.dma_start(out=KTb[b], in_=kv_d4[j, :, m, :]),
                       r=[("ktd", j)], w=[("KTb", b)], dma=f"ktb{b}")
                    op("sp", lambda e: e.dma_start(out=Vb[b].rearrange("p s e -> p (s e)"), in_=v_d4[j, :, hd, :]),
                       r=[("vd", j)], w=VbK[b], dma=f"vb{b}")
                b = cur["b"]
                diag = (j == tt)
                q0c = kb * 128 if diag else 0
                lb_ = cnt["l"] % 2
                pb = cnt["l"] % 2
                cnt["l"] += 1
                op("pe", lambda e: e.matmul(ps[lb_][:, q0c:TT], lhsT=KTb[b][:, kb * 128:(kb + 1) * 128], rhs=act[:, 16 + m, q0c:TT], start=True, stop=True),
                   r=[("KTb", b), AK(16 + m)], w=[PK(lb_)])
                op("act", lambda e: e.activation(out=probs[pb][:, q0c:TT], in_=ps[lb_][:, q0c:TT], func=AF.Exp, scale=QSCALE),
                   r=[PK(lb_)], w=[("probs", pb)])
                if diag:
                    op("pool", lambda e: e.tensor_tensor(out=probs[pb][:, kb * 128:(kb + 1) * 128], in0=probs[pb][:, kb * 128:(kb + 1) * 128],
                                                         in1=causal[:], op=ALU.mult), r=[("probs", pb), "causal"], w=[("probs", pb)])
                return dict(b=b, pb=pb, diag=diag)

            deferred = []

            def emit_O(it, info):
                hd, c, j, kb = it
                b, pb, diag = info["b"], info["pb"], info["diag"]
                for qb in range(kb if diag else 0, 4):
                    first = (j == 0 and kb == 0)
                    last = (diag and kb == qb)
                    op("pe", lambda e, qb=qb, first=first, last=last: e.matmul(
                        ps[obank[qb]][:, 0:257], lhsT=probs[pb][:, qb * 128:(qb + 1) * 128], rhs=Vb[b][:, kb, :], start=first, stop=last, skip_group_check=True),
                       r=[("probs", pb)] + VbK[b], w=[PK(obank[qb])])
                if not (j == tt and kb == 3):
                    return
                for qb in range(4):
                    ob = obank[qb]
                    op("dve", lambda e, qb=qb, ob=ob: e.reciprocal(out=rden[:, qb:qb + 1], in_=ps[ob][:, 256:257]), r=[PK(ob)], w=[("rden", qb)])
                    if c == 0:
                        op("dve", lambda e, qb=qb, ob=ob: e.tensor_scalar(out=o1[:, qb, :], in0=ps[ob][:, 0:256], scalar1=rden[:, qb:qb + 1], scalar2=None, op0=ALU.mult),
                           r=[PK(ob), ("rden", qb)], w=[("o1", qb)])
                    else:
                        op("dve", lambda e, qb=qb: e.tensor_scalar(out=rden[:, qb:qb + 1], in0=rden[:, qb:qb + 1], scalar1=NEGLAM, scalar2=None, op0=ALU.mult),
                           r=[("rden", qb), "small"], w=[("rden", qb)])
                        op("dve", lambda e, qb=qb, ob=ob: e.scalar_tensor_tensor(out=o1[:, qb, :], in0=ps[ob][:, 0:256], scalar=rden[:, qb:qb + 1], in1=o1[:, qb, :],
                                                                                 op0=ALU.mult, op1=ALU.add), r=[PK(ob), ("rden", qb), ("o1", qb)], w=[("o1", qb)])
                if c == 0:
                    return
                O1K = [("o1", qb) for qb in range(4)]
                op("act", lambda e: e.activation(out=o2, in_=o1, func=AF.Square), r=O1K, w=[("o2",)])
                op("dve", lambda e: e.reduce_sum(out=ssq, in_=o2, axis=AX.X), r=[("o2",)], w=[("ssq",)])
                op("act", lambda e: e.activation(out=ssq, in_=ssq, func=AF.Ln, scale=1.0 / 256, bias=epsc[:, 0:1]), r=[("ssq",), "epsc"], w=[("ssq",)])
                op("act", lambda e: e.activation(out=ssq, in_=ssq, func=AF.Exp, scale=-0.5), r=[("ssq",)], w=[("ssq",)])
                for qb in range(4):
                    op("dve", lambda e, qb=qb: e.scalar_tensor_tensor(out=yb[:, qb, :], in0=o1[:, qb, :], scalar=ssq[:, qb:qb + 1], in1=sublnb[:],
                                                                       op0=ALU.mult, op1=ALU.mult), r=[("o1", qb), ("ssq",), "sublnb"], w=[("yb", qb)])

                def tail(hd=hd):
                    for qb in range(4):
                        for e2 in range(2):
                            op("pe", lambda e, qb=qb, e2=e2: e.transpose(p7b[:, e2, qb * 128:(qb + 1) * 128], yb[:, qb, e2 * 128:(e2 + 1) * 128], ident[:]),
                               r=[("yb", qb), "ident"], w=[PK(7)])
                    op("act", lambda e: e.activation(out=act[:, 2 * hd:2 * hd + 2, :], in_=p7b, func=AF.Copy), r=[PK(7)], w=[AK(2 * hd), AK(2 * hd + 1)])
                deferred.append([3, tail])

            prev = None
            for it in iters:
                info = emit_L(it)
                if prev is not None:
                    emit_O(*prev)
                prev = (it, info)
                for dfr in list(deferred):
                    dfr[0] -= 1
                    if dfr[0] <= 0:
                        deferred.remove(dfr)
                        dfr[1]()
            emit_O(*prev)
            for dfr in deferred:
                dfr[1]()

        IDX = {nm: i for i, (nm, t) in reversed(list(enumerate(WLIST)))}
        XK = [("x", c) for c in range(NCH)]
        done = False
        for tt in range(NTT):
            t0 = tt * TT
            base = tt * NT
            op("sp", lambda e, t0=t0: e.dma_start(out=xT[:], in_=xT_v[:, :, t0:t0 + TT]), w=XK, dma="x")
            stages = []
            if stop_after != "load":
                norm_fm(G_MIX0)
            nh = {"load": 0, "norm": 0, "h1": 1}.get(stop_after, 16)
            hgrn_all(base + IDX["ainA"], nh)
            if nh == 16:
                linear_fm(base + IDX["aout"], 8, 2, 16, src_act, evac_resid)
            if stop_after not in ("mix0", "load", "norm", "h1"):
                mlp(base + IDX["up0"], G_MLP0)
            if stop_after not in ("mix0", "mlp0", "load", "norm", "h1"):
                ple(base + IDX["pp0"], G_PLE0, 0, t0)
            if stop_after is None or stop_after in ("mix1", "mlp1"):
                kv_stage(base + IDX["wk"], tt)
                attn_stage(base + IDX["bq"], tt)
                linear_fm(base + IDX["bout"], 8, 2, 16, src_act, evac_resid)
            if stop_after is None or stop_after == "mlp1":
                mlp(base + IDX["up1"], G_MLP1)
            if stop_after is None:
                ple(base + IDX["pp1"], G_PLE1, 1, t0)
            op("sp", lambda e, t0=t0: e.dma_start(out=out_v[:, :, t0:t0 + TT], in_=xT[:]), r=XK, dma="out")

        run = S_.emit(sem)
        with nc.Block() as block:
            @block.sync
            def _(eng):
                run("sp", eng)
                for s, v in S_.final_counts[1].items():
                    eng.wait_ge(sem(s), v)

            @block.scalar
            def _(eng):
                run("act", eng)

            @block.vector
            def _(eng):
                run("dve", eng)

            @block.gpsimd
            def _(eng):
                run("pool", eng)

            @block.tensor
            def _(eng):
                run("pe", eng)
    nc._n_ops = len(S_.ops)
    return nc


def host_prep(inputs):
    inp = {k: np.asarray(v) for k, v in inputs.items()}
    wf = host_weight_tiles(inp)
    vecs = host_vecs(inp)
    return inp, wf, vecs


def kernel(**inputs):
    inp, wf, vecs = host_prep(inputs)
    x = inp["x"]
    p = inp["p"]
    B, S, _ = x.shape
    nc = build_nc(S)
    in_maps = []
    for b in range(B):
        in_maps.append({
            "xT": np.ascontiguousarray(x[b].T),
            "pT": np.ascontiguousarray(p[:, b].transpose(0, 2, 1)),
            "wf": wf,
            "vecs": vecs,
        })
    res = run_bass_kernel_spmd(nc, in_maps, core_ids=list(range(B)))
    out = np.empty((B, S, D), np.float32)
    for b in range(B):
        out[b] = res.results[b]["outT"].T
    return out
```

```python
import math
from contextlib import ExitStack
import numpy as np
import concourse.bass as bass
import concourse.mybir as mybir
from concourse.bass_utils import run_bass_kernel_spmd

F32 = mybir.dt.float32
BF16 = mybir.dt.bfloat16
AF = mybir.ActivationFunctionType
ALU = mybir.AluOpType
AX = mybir.AxisListType

D = 2048
NCH = 16
TT = 512
DFF = 8192
EPS = 1e-6
WT = 4096
NSLOT = 6
CASTG = 4
LAM_INIT = 0.8 - 0.6 * math.exp(-0.3 * 1)
QSCALE = 1.0 / math.sqrt(128.0)

V_GAIN = 0
V_LB = 112
V_ONORM = 160
V_KN = 161
V_QN = 163
V_LAM = 165
V_SUBLN = 677
NV = 933
G_MIX0, G_MLP0, G_PLE0, G_KV, G_MIX1, G_MLP1, G_PLE1 = range(7)


class Sched:
    ENGS = ("pe", "act", "dve", "pool", "sp")

    def __init__(self):
        self.ops = []
        self.last_w = {}
        self.readers = {}

    def op(self, eng, fn, r=(), w=(), dma=None, waw_ok=False):
        i = len(self.ops)
        deps = set()
        for k in r:
            deps.update(self.last_w.get(k, ()))
        for k in w:
            rd = self.readers.get(k, [])
            lw = self.last_w.get(k, [])
            if rd:
                deps.update(rd)
                deps.update(lw)
                self.last_w[k] = [i]
                self.readers[k] = []
            elif waw_ok:
                self.last_w.setdefault(k, []).append(i)
            else:
                deps.update(lw)
                self.last_w[k] = [i]
        for k in r:
            if k not in w:
                self.readers.setdefault(k, []).append(i)
        deps.discard(i)
        self.ops.append(dict(eng=eng, fn=fn, deps=deps, dma=dma, isdma=dma is not None))
        return i

    def emit(self, sem_ctx):
        ops = self.ops
        n = len(ops)
        needed = [False] * n
        for o in ops:
            keep = set()
            best = {}
            for d in o["deps"]:
                od = ops[d]
                if od["isdma"]:
                    keep.add(d)
                    continue
                if od["eng"] == "pe" and o["eng"] == "pe" and not o["isdma"]:
                    continue
                if d > best.get(od["eng"], -1):
                    best[od["eng"]] = d
            keep.update(best.values())
            o["deps"] = keep
            for d in keep:
                needed[d] = True
        eng_cnt = {e: 0 for e in self.ENGS}
        dma_cnt = {}
        EPOCH = 4000
        DEPOCH = 200
        finals = {}
        for i, o in enumerate(ops):
            if o["isdma"]:
                s = o["dma"]
                c = dma_cnt.get(s, 0)
                dma_cnt[s] = c + 1
                name = f"dma:{s}:{c // DEPOCH}"
                o["sig"] = (name, (c % DEPOCH + 1) * 16)
                finals[name] = o["sig"][1]
            elif needed[i]:
                c = eng_cnt[o["eng"]]
                eng_cnt[o["eng"]] = c + 1
                o["sig"] = (f"eng:{o['eng']}:{c // EPOCH}", c % EPOCH + 1)
            else:
                o["sig"] = None
        self.final_counts = (eng_cnt, finals)
        streams = {e: [] for e in self.ENGS}
        for i, o in enumerate(ops):
            streams[o["eng"]].append(i)

        def run_stream(e, engobj):
            waited = {}
            for i in streams[e]:
                o = ops[i]
                need = {}
                for d in o["deps"]:
                    sname, val = ops[d]["sig"]
                    if val > need.get(sname, 0):
                        need[sname] = val
                for sname, val in need.items():
                    if waited.get(sname, 0) >= val:
                        continue
                    engobj.wait_ge(sem_ctx(sname), val)
                    waited[sname] = val
                ins = o["fn"](engobj)
                if o["sig"] is not None:
                    sname, val = o["sig"]
                    ins.then_inc(sem_ctx(sname), 16 if o["isdma"] else 1)

        return run_stream


def weight_tile_list():
    L = []
    for h in range(16):
        L.append(("ainA", h))
        L.append(("ainB", h))
    for t in range(8):
        L.append(("aout", t))
    for half in range(2):
        for t in range(16):
            L.append(("up0", half * 16 + t))
        for t in range(16):
            L.append(("dn0", half * 16 + t))
    L.append(("pp0", 0))
    for t in range(8):
        L.append(("pg0", t))
    for t in range(8):
        L.append(("wk", t))
    for t in range(8):
        L.append(("wv", t))
    for t in range(8):
        L.append(("bq", t))
    for t in range(8):
        L.append(("bout", t))
    for half in range(2):
        for t in range(16):
            L.append(("up1", half * 16 + t))
        for t in range(16):
            L.append(("dn1", half * 16 + t))
    L.append(("pp1", 0))
    for t in range(8):
        L.append(("pg1", t))
    return L


WLIST = weight_tile_list()
NT = len(WLIST)
NTP = ((NT + CASTG - 1) // CASTG) * CASTG


def _tile_kn(W, k0, kc, cols):
    sub = W[k0:k0 + kc * 128][:, cols]
    n = sub.shape[1]
    return np.ascontiguousarray(sub.reshape(kc, 128, n).transpose(1, 0, 2)).reshape(128, kc * n)


def host_weight_tiles(inp):
    out = np.zeros((NTP, 128, WT), np.float32)
    ar = np.arange
    for i, (nm, t) in enumerate(WLIST):
        if nm == "ainA":
            W = inp["a_w_in"][0]
            cols = np.concatenate([ar(t * 128, t * 128 + 128), ar(2048 + t * 128, 2048 + t * 128 + 128)])
            out[i] = _tile_kn(W, 0, 16, cols)
        elif nm == "ainB":
            W = inp["a_w_in"][0]
            cols = np.concatenate([ar(4096 + t * 128, 4096 + t * 128 + 128), ar(6144 + t * 128, 6144 + t * 128 + 128)])
            out[i] = _tile_kn(W, 0, 16, cols)
        elif nm in ("aout", "wk", "wv", "bq", "bout", "pg0", "pg1"):
            W = {"aout": inp["a_w_out"][0], "wk": inp["w_k"], "wv": inp["w_v"], "bq": inp["b_w_q"][0],
                 "bout": inp["b_w_out"][0], "pg0": inp["ple_gate"][0], "pg1": inp["ple_gate"][1]}[nm]
            out[i] = _tile_kn(W, 0, 16, ar(t * 256, t * 256 + 256))
        elif nm in ("up0", "up1"):
            W = inp["mlp_up"][int(nm[2])]
            out[i] = _tile_kn(W, 0, 16, ar(t * 256, t * 256 + 256))
        elif nm in ("dn0", "dn1"):
            W = inp["mlp_down"][int(nm[2])]
            half, m = divmod(t, 16)
            out[i] = _tile_kn(W, half * 4096, 32, ar(m * 128, m * 128 + 128))
        elif nm in ("pp0", "pp1"):
            W = inp["ple_proj"][int(nm[2])]
            out[i] = _tile_kn(W, 0, 2, ar(0, 2048))
    return out


def host_vecs(inp):
    v = np.zeros((128, NV), np.float32)
    gains = [inp["ln_mix"][0], inp["ln_mlp"][0], inp["ln_ple"][0], inp["kv_norm"],
             inp["ln_mix"][1], inp["ln_mlp"][1], inp["ln_ple"][1]]
    for i, g in enumerate(gains):
        v[:, V_GAIN + i * 16:V_GAIN + (i + 1) * 16] = g.reshape(16, 128).T
    for r in range(3):
        v[:, V_LB + r * 16:V_LB + (r + 1) * 16] = inp["a_lb"][r].reshape(16, 128).T
    v[:, V_ONORM] = inp["a_onorm"][0]
    v[:, V_KN] = inp["k_norm"][0]
    v[:, V_KN + 1] = inp["k_norm"][1]
    v[:, V_QN] = inp["q_norm"][0, 0]
    v[:, V_QN + 1] = inp["q_norm"][0, 1]
    lam = np.concatenate([inp["lam_q1"][0], inp["lam_k1"][0], inp["lam_q2"][0], inp["lam_k2"][0]])
    v[:, V_LAM:V_LAM + 512] = lam[None, :]
    v[:, V_SUBLN:V_SUBLN + 256] = inp["b_subln"][0][None, :]
    return v


def build_nc(S, stop_after=None):
    import os
    HSTOP = int(os.environ.get('DBG_HSTOP', 99))
    NTT = S // TT
    nc = bass.Bass("TRN2", target_bir_lowering=False)
    xT_d = nc.dram_tensor("xT", [D, S], F32, kind="ExternalInput").ap()
    pT_d = nc.dram_tensor("pT", [2, 256, S], F32, kind="ExternalInput").ap()
    wf_d = nc.dram_tensor("wf", [NTP, 128, WT], F32, kind="ExternalInput").ap()
    vec_d = nc.dram_tensor("vecs", [128, NV], F32, kind="ExternalInput").ap()
    out_d = nc.dram_tensor("outT", [D, S], F32, kind="ExternalOutput").ap()
    wb_d = nc.dram_tensor("wb", [NTP, 128, WT], BF16, kind="Internal").ap()
    kt_d = nc.dram_tensor("ktc", [NTT, 128, 16 * TT], BF16, kind="Internal").ap()
    v_d = nc.dram_tensor("vc", [NTT, 128, 8 * 4 * 257], BF16, kind="Internal").ap()

    xT_v = xT_d.rearrange("(c p) t -> p c t", p=128)
    out_v = out_d.rearrange("(c p) t -> p c t", p=128)

    S_ = Sched()
    op = S_.op
    es = ExitStack()
    with es:
        def sb(name, shape, dt):
            return es.enter_context(nc.sbuf_tensor(name, shape, dt))

        xT = sb("xTs", [128, NCH, TT], F32)
        hT = sb("hT", [128, NCH, TT], BF16)
        sqb = sb("sqb", [128, 4, TT], BF16)
        rstd = sb("rstd", [128, TT], F32)
        rstdB = sb("rstdB", [128, TT], F32)
        wr = sb("wring", [128, NSLOT, WT], BF16)
        wpj = sb("wpj", [128, WT], BF16)
        act = sb("actT", [128, 32, TT], BF16)
        U = sb("U", [128, 20, TT], F32)
        Scar = sb("Scar", [128, 16, 128], F32)
        vecs = sb("vecs_s", [128, NV], F32)
        pTb = sb("pTb", [128, 2, TT], BF16)
        pTf = sb("pTf", [128, 2, TT], F32)
        ident = sb("ident", [128, 128], BF16)
        ones = sb("ones", [128, 128], BF16)
        causal = sb("causal", [128, 128], BF16)
        maskbd = sb("maskbd", [128, TT], F32)
        scanm = sb("scanm", [128, TT], F32)
        small = sb("small", [128, 64], F32)
        epsc = sb("epsc", [128, 1], F32)
        onec = sb("onec", [128, 1], F32)
        hmask = sb("hmask", [128, 2], F32)
        lbt = U[:, 19, 0:48]
        lamt = U[:, 18, 0:256]
        sublnb = sb("sublnb", [128, 256], F32)
        ps = [es.enter_context(nc.psum_tensor(f"ps{i}", [128, 512], F32)) for i in range(8)]

        sems = {}

        def sem(name):
            if name not in sems:
                sems[name] = es.enter_context(nc.semaphore(name.replace(":", "_")))
            return sems[name]

        def Uf(i, n=1):
            if n == 1:
                return U[:, i, :]
            return U[:, i:i + n, :]

        def Ub(i):
            return U[:, i, :].bitcast(BF16)

        UK = lambda i: ("U", i)
        PK = lambda i: ("ps", i)
        AK = lambda i: ("act", i)

        op("sp", lambda e: e.dma_start(out=vecs[:], in_=vec_d), w=["vecs"], dma="vec")
        actf = act[:].rearrange("p a t -> p (a t)").bitcast(F32)
        Uflat = U[:].rearrange("p a t -> p (a t)")
        stages = [(Uflat[:, 0:WT], [("U", i) for i in range(0, 8)]),
                  (Uflat[:, WT:2 * WT], [("U", i) for i in range(8, 16)]),
                  (actf[:, 0:WT], [("act", i) for i in range(0, 16)]),
                  (actf[:, WT:2 * WT], [("act", i) for i in range(16, 32)])]
        import os
        ncast = int(os.environ.get("DBG_NCAST", NT))
        def cast_ld(i):
            stg, skeys = stages[i % 4]
            op("sp", lambda e: e.dma_start(out=stg, in_=wf_d[i]), w=skeys, dma=f"cl{i % 4}")
        for i in range(min(4, ncast)):
            cast_ld(i)
        for i in range(ncast):
            stg, skeys = stages[i % 4]
            slot = i % NSLOT
            if i % 2 == 0:
                op("act", lambda e, stg=stg, slot=slot: e.activation(out=wr[:, slot, :], in_=stg, func=AF.Copy), r=skeys, w=[("w", slot)])
            else:
                op("dve", lambda e, stg=stg, slot=slot: e.tensor_copy(out=wr[:, slot, :], in_=stg), r=skeys, w=[("w", slot)])
            op("sp", lambda e, i=i, slot=slot: e.dma_start(out=wb_d[i], in_=wr[:, slot, :]), r=[("w", slot)], w=[("wb", i)], dma=f"cs{slot}")
            if i + 4 < ncast:
                cast_ld(i + 4)
        op("dve", lambda e: e.memset(ones[:], 1.0), w=["ones"])
        op("dve", lambda e: e.memset(epsc[:], EPS), w=["epsc"])
        op("dve", lambda e: e.memset(onec[:], 1.0), w=["onec"])
        op("dve", lambda e: e.memset(ident[:], 1.0), w=["ident"])
        op("pool", lambda e: e.affine_select(out=ident[:], in_=ident[:], pattern=[[-1, 128]], compare_op=ALU.is_equal,
                                              fill=0.0, base=0, channel_multiplier=1), r=["ident"], w=["ident"])
        op("dve", lambda e: e.memset(causal[:], 1.0), w=["causal"])
        op("pool", lambda e: e.affine_select(out=causal[:], in_=causal[:], pattern=[[1, 128]], compare_op=ALU.is_ge,
                                              fill=0.0, base=0, channel_multiplier=-1), r=["causal"], w=["causal"])
        op("dve", lambda e: e.memset(maskbd[:], 1.0), w=["maskbd"])
        mb3 = maskbd[:].rearrange("p (j t) -> p j t", t=128)
        op("pool", lambda e: e.affine_select(out=mb3, in_=mb3, pattern=[[0, 4], [1, 128]], compare_op=ALU.is_ge,
                                              fill=0.0, base=0, channel_multiplier=-1), r=["maskbd"], w=["maskbd"])
        op("dve", lambda e: e.memset(scanm[:], 1.0), w=["scanm"])
        sm3 = scanm[:].rearrange("p (c t) -> p c t", t=128)
        op("dve", lambda e: e.memset(sm3[:, :, 0:1], 0.0), r=["scanm"], w=["scanm"])
        op("dve", lambda e: e.memset(Scar[:], 0.0), w=["Scar"])
        op("act", lambda e: e.activation(out=lbt, in_=vecs[:, V_LB:V_LB + 48], func=AF.Exp), r=["vecs"], w=["lbt"])
        op("dve", lambda e: e.tensor_tensor(out=small[:, 32:48], in0=lbt[:, 0:16], in1=lbt[:, 16:32], op=ALU.add), r=["lbt"], w=["small"])
        op("dve", lambda e: e.tensor_tensor(out=small[:, 32:48], in0=small[:, 32:48], in1=lbt[:, 32:48], op=ALU.add), r=["lbt", "small"], w=["small"])
        op("dve", lambda e: e.reciprocal(out=small[:, 32:48], in_=small[:, 32:48]), r=["small"], w=["small"])
        op("dve", lambda e: e.tensor_tensor(out=small[:, 0:16], in0=lbt[:, 0:16], in1=small[:, 32:48], op=ALU.mult), r=["lbt", "small"], w=["small"])
        op("dve", lambda e: e.tensor_scalar(out=small[:, 16:32], in0=small[:, 0:16], scalar1=-1.0, scalar2=1.0, op0=ALU.mult, op1=ALU.add), r=["small"], w=["small"])
        op("dve", lambda e: e.tensor_tensor(out=lamt[:, 0:128], in0=vecs[:, V_LAM:V_LAM + 128], in1=vecs[:, V_LAM + 128:V_LAM + 256], op=ALU.mult), r=["vecs"], w=["lamt"])
        op("dve", lambda e: e.tensor_tensor(out=lamt[:, 128:256], in0=vecs[:, V_LAM + 256:V_LAM + 384], in1=vecs[:, V_LAM + 384:V_LAM + 512], op=ALU.mult), r=["vecs", "lamt"], w=["lamt"])
        op("dve", lambda e: e.reduce_sum(out=small[:, 49:51], in_=lamt.rearrange("p (a b) -> p a b", b=128), axis=AX.X), r=["lamt", "small"], w=["small"])
        op("act", lambda e: e.activation(out=small[:, 49:51], in_=small[:, 49:51], func=AF.Exp), r=["small"], w=["small"])
        op("dve", lambda e: e.tensor_tensor(out=small[:, 48:49], in0=small[:, 50:51], in1=small[:, 49:50], op=ALU.subtract), r=["small"], w=["small"])
        op("dve", lambda e: e.tensor_scalar(out=small[:, 48:49], in0=small[:, 48:49], scalar1=-LAM_INIT, scalar2=None, op0=ALU.add), r=["small"], w=["small"])
        op("dve", lambda e: e.tensor_scalar(out=sublnb[:], in0=vecs[:, V_SUBLN:V_SUBLN + 256], scalar1=1.0 - LAM_INIT, scalar2=None, op0=ALU.mult), r=["vecs"], w=["sublnb"])
        LB = lambda h: small[:, h:h + 1]
        OML = lambda h: small[:, 16 + h:17 + h]
        NEGLAM = small[:, 48:49]

        wstate = dict(issued=0)

        def w_issue_upto(n):
            while wstate["issued"] <= n:
                q = wstate["issued"]
                gi = q % NT
                if WLIST[gi][0] in ("pp0", "pp1"):
                    wstate["issued"] += 1
                    continue
                slot = q % NSLOT
                op("sp", lambda e, gi=gi, slot=slot: e.dma_start(out=wr[:, slot, :], in_=wb_d[gi]),
                   r=[("wb", gi)], w=[("w", slot)], dma=f"w{slot}")
                wstate["issued"] += 1

        def w_get(q):
            w_issue_upto(min(q + NSLOT - 1, NTT * NT - 1))
            return q % NSLOT

        ctr = dict(bank=0, alt=0)

        def alt(a, b):
            ctr["alt"] += 1
            return a if ctr["alt"] % 2 else b

        def linear_fm(q0, ntiles, mper, kc, src, evac, banks=(0, 1, 2, 3)):
            ncols = WT // kc
            for ti in range(ntiles):
                slot = w_get(q0 + ti)
                for j in range(mper):
                    m = ti * mper + j
                    bank = banks[ctr["bank"] % len(banks)]
                    ctr["bank"] += 1
                    for k in range(kc):
                        rhs, key = src(k)
                        op("pe", lambda e, bank=bank, slot=slot, k=k, j=j, rhs=rhs: e.matmul(
                            ps[bank][:], lhsT=wr[:, slot, k * ncols + j * 128:k * ncols + (j + 1) * 128], rhs=rhs,
                            start=(k == 0), stop=(k == kc - 1)), r=[("w", slot), key], w=[PK(bank)])
                    evac(m, bank)

        def norm_fm(gidx):
            for c in range(NCH):
                s = c % 4
                if c % 3 == 0:
                    op("act", lambda e, c=c, s=s: e.activation(out=sqb[:, s, :], in_=xT[:, c, :], func=AF.Square),
                       r=[("x", c)], w=[("sqb", s)])
                elif c % 3 == 1:
                    op("dve", lambda e, c=c, s=s: e.tensor_tensor(out=sqb[:, s, :], in0=xT[:, c, :], in1=xT[:, c, :], op=ALU.mult),
                       r=[("x", c)], w=[("sqb", s)])
                else:
                    op("pool", lambda e, c=c, s=s: e.tensor_tensor(out=sqb[:, s, :], in0=xT[:, c, :], in1=xT[:, c, :], op=ALU.mult),
                       r=[("x", c)], w=[("sqb", s)])
                op("pe", lambda e, c=c, s=s: e.matmul(ps[4][:], lhsT=ones[:], rhs=sqb[:, s, :], start=(c == 0), stop=(c == NCH - 1)),
                   r=["ones", ("sqb", s)], w=[PK(4)])
            op("act", lambda e: e.activation(out=rstd[:], in_=ps[4][:], func=AF.Ln, scale=1.0 / D, bias=epsc[:, 0:1]), r=[PK(4), "epsc"], w=["rstd"])
            op("act", lambda e: e.activation(out=rstd[:], in_=rstd[:], func=AF.Exp, scale=-0.5), r=["rstd"], w=["rstd"])
            for c in range(NCH):
                col = V_GAIN + gidx * 16 + c
                op("dve", lambda e, c=c, col=col: e.scalar_tensor_tensor(
                    out=hT[:, c, :], in0=xT[:, c, :], scalar=vecs[:, col:col + 1], in1=rstd[:], op0=ALU.mult, op1=ALU.mult),
                   r=[("x", c), "vecs", "rstd"], w=[("h", c)])

        def src_h(k):
            return hT[:, k, :], ("h", k)

        def src_act(k):
            return act[:, k, :], AK(k)

        def evac_resid(m, bank):
            op("dve", lambda e: e.tensor_tensor(out=xT[:, m, :], in0=xT[:, m, :], in1=ps[bank][:], op=ALU.add),
               r=[PK(bank), ("x", m)], w=[("x", m)])

        def hgrn_pieces(q0, h):
            par = h % 2
            f_, k_, lg_, q_, _, e1_, dm_, eq_, o_, G_ = [Uf(i) for i in range(10)]
            sg_ = Uf(4) if par == 0 else Uf(17)
            SGK = UK(4) if par == 0 else UK(17)
            qt = Ub(10)[:, 0:TT]
            kt = Ub(10)[:, TT:2 * TT]
            khat = Ub(11)[:, 0:TT]
            khatT = Ub(11)[:, TT:2 * TT]
            scm = Ub(12)[:, 0:TT]
            vtok = Ub(12)[:, TT:2 * TT] if par == 0 else Ub(14)[:, TT:2 * TT]
            VK = ("vtok", par)
            osq = Ub(13)[:, 0:TT]
            qS = Ub(13)[:, TT:2 * TT]
            Sb = Ub(14)[:, 0:TT]
            Sall = U[:, 15:17, :].rearrange("p a t -> p (a t)")
            egl = small[:, 52:56]
            e3_ = lg_
            st = {}

            def mm_fm(bank, slot, colofs):
                for k in range(NCH):
                    op("pe", lambda e, k=k: e.matmul(ps[bank][:], lhsT=wr[:, slot, k * 256 + colofs:k * 256 + colofs + 128],
                                                     rhs=hT[:, k, :], start=(k == 0), stop=(k == NCH - 1)),
                       r=[("w", slot), ("h", k)], w=[PK(bank)])

            def Pq():
                st["sA"] = w_get(q0 + 2 * h)
                st["sB"] = (q0 + 2 * h + 1) % NSLOT
                mm_fm(0, st["sA"], 0)
                op("act", lambda e: e.activation(out=q_, in_=ps[0][:], func=AF.Exp, scale=-1.0), r=[PK(0)], w=[UK(3)])
                op("act", lambda e: e.activation(out=q_, in_=q_, func=AF.Ln, bias=onec[:, 0:1]), r=[UK(3), "onec"], w=[UK(3)])
                op("act", lambda e: e.activation(out=q_, in_=q_, func=AF.Exp, scale=-1.0), r=[UK(3)], w=[UK(3)])
                op("dve", lambda e: e.tensor_tensor(out=q_, in0=q_, in1=ps[0][:], op=ALU.mult), r=[UK(3), PK(0)], w=[UK(3)])

            def Pf():
                mm_fm(1, st["sA"], 128)
                op("act", lambda e: e.activation(out=f_, in_=ps[1][:], func=AF.Exp, scale=-1.0), r=[PK(1)], w=[UK(0)])
                op("act", lambda e: e.activation(out=f_, in_=f_, func=AF.Ln, bias=onec[:, 0:1]), r=[UK(0), "onec"], w=[UK(0)])
                op("act", lambda e: e.activation(out=f_, in_=f_, func=AF.Exp, scale=-1.0), r=[UK(0)], w=[UK(0)])

            def Pv():
                sB = st["sB"]
                for sub in range(4):
                    for k in range(NCH):
                        op("pe", lambda e, k=k, sub=sub: e.matmul(ps[3][:, sub * 128:(sub + 1) * 128], lhsT=hT[:, k, sub * 128:(sub + 1) * 128],
                                                                  rhs=wr[:, sB, k * 256:k * 256 + 128], start=(k == 0), stop=(k == NCH - 1)),
                           r=[("w", sB), ("h", k)], w=[PK(3)])
                op("act", lambda e: e.activation(out=vtok, in_=ps[3][:], func=AF.Copy), r=[PK(3)], w=[VK])

            def Pg():
                mm_fm(1, st["sB"], 128)
                op("act", lambda e: e.activation(out=sg_, in_=ps[1][:], func=AF.Exp, scale=-1.0), r=[PK(1)], w=[SGK])
                op("act", lambda e: e.activation(out=sg_, in_=sg_, func=AF.Ln, bias=onec[:, 0:1]), r=[SGK, "onec"], w=[SGK])
                op("act", lambda e: e.activation(out=sg_, in_=sg_, func=AF.Exp, scale=-1.0), r=[SGK], w=[SGK])
                op("dve", lambda e: e.tensor_tensor(out=sg_, in0=sg_, in1=ps[1][:], op=ALU.mult), r=[SGK, PK(1)], w=[SGK])

            def RA():
                op("pool", lambda e: e.tensor_scalar(out=f_, in0=f_, scalar1=OML(h), scalar2=LB(h), op0=ALU.mult, op1=ALU.add),
                   r=[UK(0), "small"], w=[UK(0)])
                op("pool", lambda e: e.tensor_scalar(out=k_, in0=f_, scalar1=-1.0, scalar2=1.0, op0=ALU.mult, op1=ALU.add),
                   r=[UK(0)], w=[UK(1)])
                op("act", lambda e: e.activation(out=lg_, in_=f_, func=AF.Ln), r=[UK(0)], w=[UK(2)])
                op("dve", lambda e: e.tensor_tensor_scan(out=G_, data0=scanm[:], data1=lg_, initial=0.0, op0=ALU.mult, op1=ALU.add),
                   r=["scanm", UK(2)], w=[UK(9)])
                G3 = G_.rearrange("p (c t) -> p c t", t=128)
                op("act", lambda e: e.activation(out=e1_, in_=G_, func=AF.Exp), r=[UK(9)], w=[UK(5)])
                dm3 = dm_.rearrange("p (c t) -> p c t", t=128)
                op("pool", lambda e: e.tensor_tensor(out=dm3, in0=G3, in1=G3[:, :, 63:64].to_broadcast([128, 4, 128]), op=ALU.subtract),
                   r=[UK(9)], w=[UK(6)])
                op("act", lambda e: e.activation(out=eq_, in_=dm_, func=AF.Exp), r=[UK(6)], w=[UK(7)])
                op("act", lambda e: e.activation(out=dm_, in_=dm_, func=AF.Exp, scale=-1.0), r=[UK(6), UK(7)], w=[UK(6)])
                e33 = e3_.rearrange("p (c t) -> p c t", t=128)
                op("pool", lambda e: e.tensor_tensor(out=e33, in0=G3[:, :, 127:128].to_broadcast([128, 4, 128]), in1=G3, op=ALU.subtract),
                   r=[UK(9), UK(2)], w=[UK(2)])
                op("act", lambda e: e.activation(out=e3_, in_=e3_, func=AF.Exp), r=[UK(2)], w=[UK(2)])
                op("act", lambda e: e.activation(out=egl.unsqueeze(2), in_=G3[:, :, 127:128], func=AF.Exp), r=[UK(9)], w=["egl"])
                op("dve", lambda e: e.tensor_tensor(out=qS, in0=q_, in1=e1_, op=ALU.mult), r=[UK(3), UK(5)], w=[("qS",)])
                op("dve", lambda e: e.tensor_tensor(out=qt, in0=q_, in1=eq_, op=ALU.mult), r=[UK(3), UK(7)], w=[("qt",)])
                op("pool", lambda e: e.tensor_tensor(out=kt, in0=k_, in1=dm_, op=ALU.mult), r=[UK(1), UK(6)], w=[("kt",)])
                op("pool", lambda e: e.tensor_tensor(out=khat, in0=k_, in1=e3_, op=ALU.mult), r=[UK(1), UK(2)], w=[("khat",)])

            def RB():
                p5b = ps[5][:].bitcast(BF16)
                for j in range(4):
                    op("pe", lambda e, j=j: e.transpose(p5b[:, j * 128:(j + 1) * 128], khat[:, j * 128:(j + 1) * 128], ident[:]),
                       r=[("khat",), "ident"], w=[PK(5)])
                op("act", lambda e: e.activation(out=khatT, in_=p5b[:, 0:TT], func=AF.Copy), r=[PK(5)], w=[("khatT",)])

            def RC():
                for j in range(4):
                    op("pe", lambda e, j=j: e.matmul(ps[6][:, j * 128:(j + 1) * 128], lhsT=kt[:, j * 128:(j + 1) * 128],
                                                     rhs=qt[:, j * 128:(j + 1) * 128], start=True, stop=True),
                       r=[("kt",), ("qt",)], w=[PK(6)])
                op("dve", lambda e: e.tensor_tensor(out=scm, in0=ps[6][:], in1=maskbd[:], op=ALU.mult), r=[PK(6), "maskbd"], w=[("scm",)])

            def RD():
                for j in range(4):
                    op("pe", lambda e, j=j: e.matmul(ps[2][:, j * 128:(j + 1) * 128], lhsT=khatT[:, j * 128:(j + 1) * 128],
                                                     rhs=vtok[:, j * 128:(j + 1) * 128], start=True, stop=True),
                       r=[("khatT",), VK], w=[PK(2)])
                op("pool", lambda e: e.tensor_copy(out=Sall[:, 0:128], in_=Scar[:, h, :]), r=[("Scar", h)], w=[("Sall", 0)])
                for c in range(4):
                    op("dve", lambda e, c=c: e.scalar_tensor_tensor(
                        out=Sall[:, (c + 1) * 128:(c + 2) * 128], in0=Sall[:, c * 128:(c + 1) * 128], scalar=egl[:, c:c + 1],
                        in1=ps[2][:, c * 128:(c + 1) * 128], op0=ALU.mult, op1=ALU.add),
                       r=[("Sall", c), "egl", PK(2)], w=[("Sall", c + 1)])
                op("pool", lambda e: e.tensor_copy(out=Scar[:, h, :], in_=Sall[:, 512:640]), r=[("Sall", 4)], w=[("Scar", h)])
                op("act", lambda e: e.activation(out=Sb, in_=Sall[:, 0:512], func=AF.Copy), r=[("Sall", c) for c in range(4)], w=[("Sb",)])

            def RE():
                for j in range(4):
                    op("pe", lambda e, j=j: e.matmul(ps[7][:, j * 128:(j + 1) * 128], lhsT=vtok[:, j * 128:(j + 1) * 128],
                                                     rhs=scm[:, j * 128:(j + 1) * 128], start=True, stop=False),
                       r=[VK, ("scm",)], w=[PK(7)])
                    op("pe", lambda e, j=j: e.matmul(ps[7][:, j * 128:(j + 1) * 128], lhsT=Sb[:, j * 128:(j + 1) * 128],
                                                     rhs=qS[:, j * 128:(j + 1) * 128], start=False, stop=True),
                       r=[("Sb",), ("qS",)], w=[PK(7)])
                op("dve", lambda e: e.tensor_copy(out=o_, in_=ps[7][:]), r=[PK(7)], w=[UK(8)])
                op("act", lambda e: e.activation(out=osq, in_=o_, func=AF.Square), r=[UK(8)], w=[("osq",)])
                op("pe", lambda e: e.matmul(ps[4][:], lhsT=ones[:], rhs=osq, start=True, stop=True), r=["ones", ("osq",)], w=[PK(4)])
                op("act", lambda e: e.activation(out=G_, in_=ps[4][:], func=AF.Ln, scale=1.0 / 128, bias=epsc[:, 0:1]), r=[PK(4), "epsc"], w=[UK(9)])
                op("act", lambda e: e.activation(out=G_, in_=G_, func=AF.Exp, scale=-0.5), r=[UK(9)], w=[UK(9)])
                op("dve", lambda e: e.scalar_tensor_tensor(out=o_, in0=o_, scalar=vecs[:, V_ONORM:V_ONORM + 1], in1=G_, op0=ALU.mult, op1=ALU.mult),
                   r=[UK(8), UK(9), "vecs"], w=[UK(8)])
                op("pool", lambda e: e.tensor_tensor(out=act[:, h, :], in0=o_, in1=sg_, op=ALU.mult), r=[UK(8), SGK], w=[AK(h)])

            return (Pq, Pf, Pv, Pg), (RA, RB, RC, RD, RE)

        def hgrn_all(q0, nh):
            pcs = [hgrn_pieces(q0, h) for h in range(nh)]
            for h in range(nh):
                P, R = pcs[h]
                if h == 0:
                    for p in P:
                        p()
                R[0]()
                if h + 1 < nh:
                    Pn = pcs[h + 1][0]
                    Pn[0](); R[1](); Pn[1](); R[2](); Pn[2](); R[3](); Pn[3](); R[4]()
                else:
                    R[1](); R[2](); R[3](); R[4]()

        def mlp(q0, gidx):
            norm_fm(gidx)
            for half in range(2):
                def evac_up(m, bank):
                    t = m % 2
                    op("act", lambda e: e.activation(out=Uf(t), in_=ps[bank][:], func=AF.Relu), r=[PK(bank)], w=[UK(t)])
                    op("pool", lambda e: e.tensor_tensor(out=act[:, m, :], in0=Uf(t), in1=Uf(t), op=ALU.mult), r=[UK(t)], w=[AK(m)])
                linear_fm(q0 + half * 32, 16, 2, 16, src_h, evac_up)
                linear_fm(q0 + half * 32 + 16, 16, 1, 32, src_act, evac_resid)

        def ple(q0, gidx, layer, t0):
            gpp = q0 % NT
            op("sp", lambda e: e.dma_start(out=wpj[:], in_=wb_d[gpp]), r=[("wb", gpp)], w=["wpj"], dma="wpj")
            op("sp", lambda e: e.dma_start(out=pTf[:], in_=pT_d[layer].rearrange("(k p) t -> p k t", p=128)[:, :, t0:t0 + TT]),
               w=["pTf"], dma="pt")
            op("pool", lambda e: e.tensor_copy(out=pTb[:], in_=pTf[:]), r=["pTf"], w=["pTb"])
            norm_fm(gidx)

            def evac_gate(m, bank):
                t = 2 + m % 2
                op("act", lambda e: e.activation(out=Uf(t), in_=ps[bank][:], func=AF.Exp, scale=-1.0), r=[PK(bank)], w=[UK(t)])
                op("act", lambda e: e.activation(out=Uf(t), in_=Uf(t), func=AF.Ln, bias=onec[:, 0:1]), r=[UK(t), "onec"], w=[UK(t)])
                op("act", lambda e: e.activation(out=Uf(t), in_=Uf(t), func=AF.Exp, scale=-1.0), r=[UK(t)], w=[UK(t)])
                pb = 5 + m % 2
                for k in range(2):
                    op("pe", lambda e, k=k: e.matmul(ps[pb][:], lhsT=wpj[:, k * 2048 + m * 128:k * 2048 + (m + 1) * 128], rhs=pTb[:, k, :],
                                                     start=(k == 0), stop=(k == 1)), r=["wpj", "pTb"], w=[PK(pb)])
                op("dve", lambda e: e.tensor_tensor(out=Uf(t), in0=Uf(t), in1=ps[pb][:], op=ALU.mult), r=[UK(t), PK(pb)], w=[UK(t)])
                op("pool", lambda e: e.tensor_tensor(out=xT[:, m, :], in0=xT[:, m, :], in1=Uf(t), op=ALU.add), r=[UK(t), ("x", m)], w=[("x", m)])
            linear_fm(q0 + 1, 8, 2, 16, src_h, evac_gate)

        def headnorm_evac(bank, dst, dkey, ncol, t):
            sq = sqb[:, t, :]
            sbk = (4, 7)[t]
            rs = rstd[:] if t == 0 else rstdB[:]
            RK = "rstd" if t == 0 else "rstdB"
            op("dve", lambda e: e.tensor_copy(out=Uf(18 + t), in_=ps[bank][:]), r=[PK(bank)], w=[UK(18 + t)])
            op("act", lambda e: e.activation(out=sq, in_=Uf(18 + t), func=AF.Square), r=[UK(18 + t)], w=[("sqb", t)])
            op("pe", lambda e: e.matmul(ps[sbk][:], lhsT=ones[:], rhs=sq, start=True, stop=True), r=["ones", ("sqb", t)], w=[PK(sbk)])
            op("act", lambda e: e.activation(out=rs, in_=ps[sbk][:], func=AF.Ln, scale=1.0 / 128, bias=epsc[:, 0:1]), r=[PK(sbk), "epsc"], w=[RK])
            op("act", lambda e: e.activation(out=rs, in_=rs, func=AF.Exp, scale=-0.5), r=[RK], w=[RK])
            op("dve", lambda e: e.scalar_tensor_tensor(out=dst, in0=Uf(18 + t), scalar=vecs[:, ncol:ncol + 1], in1=rs, op0=ALU.mult, op1=ALU.mult),
               r=[UK(18 + t), RK, "vecs"], w=[dkey])

        VSTK = [UK(i) for i in range(9)] + [("KTb", 0), ("KTb", 1)]

        def kv_stage(q0, tt):
            norm_fm(G_KV)

            def evac_k(m, bank):
                headnorm_evac(bank, act[:, m, :], AK(m), V_KN + (m % 2), m % 2)
            linear_fm(q0, 8, 2, 16, src_h, evac_k, banks=(0, 1))
            op("sp", lambda e: e.dma_start(out=kt_d[tt], in_=act[:, 0:16, :].rearrange("p a t -> p (a t)")),
               r=[AK(m) for m in range(16)], w=[("ktd", tt)], dma="kts")
            Vst = U[:, 0:9, :].rearrange("p a t -> p (a t)").bitcast(BF16)[:, 0:8 * 4 * 257].rearrange("p (h s e) -> p h s e", h=8, s=4)
            op("pool", lambda e: e.memset(Vst[:, :, :, 256:257], 1.0), w=VSTK)
            for hd in range(8):
                slot = w_get(q0 + 8 + hd)
                for sub in range(4):
                    bank = (0, 1, 2, 3)[ctr["bank"] % 4]
                    ctr["bank"] += 1
                    for k in range(NCH):
                        op("pe", lambda e, k=k, sub=sub, bank=bank, slot=slot: e.matmul(
                            ps[bank][:, 0:256], lhsT=hT[:, k, sub * 128:(sub + 1) * 128], rhs=wr[:, slot, k * 256:(k + 1) * 256],
                            start=(k == 0), stop=(k == NCH - 1)), r=[("w", slot), ("h", k)], w=[PK(bank)])
                    if ctr["bank"] % 2:
                        op("dve", lambda e, hd=hd, sub=sub, bank=bank: e.tensor_copy(out=Vst[:, hd, sub, 0:256], in_=ps[bank][:, 0:256]),
                           r=[PK(bank)], w=VSTK, waw_ok=True)
                    else:
                        op("act", lambda e, hd=hd, sub=sub, bank=bank: e.activation(out=Vst[:, hd, sub, 0:256], in_=ps[bank][:, 0:256], func=AF.Copy),
                           r=[PK(bank)], w=VSTK, waw_ok=True)
            op("sp", lambda e: e.dma_start(out=v_d[tt], in_=U[:, 0:9, :].rearrange("p a t -> p (a t)").bitcast(BF16)[:, 0:8 * 4 * 257]),
               r=VSTK, w=[("vd", tt)], dma="vs")

        def attn_stage(q0, tt):
            norm_fm(G_MIX1)

            def evac_q(m, bank):
                headnorm_evac(bank, act[:, 16 + m, :], AK(16 + m), V_QN + (m % 2), m % 2)
            linear_fm(q0, 8, 2, 16, src_h, evac_q, banks=(0, 1))
            KTb = [Ub(0)[:, 0:TT], Ub(0)[:, TT:2 * TT]]
            Vb = [U[:, 2:4, :].rearrange("p a t -> p (a t)").bitcast(BF16)[:, 0:4 * 257].rearrange("p (s e) -> p s e", s=4),
                  U[:, 4:6, :].rearrange("p a t -> p (a t)").bitcast(BF16)[:, 0:4 * 257].rearrange("p (s e) -> p s e", s=4)]
            VbK = [[UK(2), UK(3)], [UK(4), UK(5)]]
            probs = [Ub(6)[:, 0:TT], Ub(6)[:, TT:2 * TT]]
            o1 = U[:, 7:9, :].rearrange("p a t -> p (a t)").rearrange("p (q e) -> p q e", q=4)
            o2 = U[:, 9:11, :].rearrange("p a t -> p (a t)").rearrange("p (q e) -> p q e", q=4)
            yb = Ub(11).rearrange("p (q e) -> p q e", q=4)
            rden = small[:, 60:64]
            ssq = small[:, 40:44]
            obank = (2, 3, 5, 6)
            kv_d4 = kt_d.rearrange("n p (m t) -> n p m t", m=16)
            v_d4 = v_d.rearrange("n p (h x) -> n p h x", h=8)
            cnt = dict(b=0, l=0)
            p7b = ps[7][:].bitcast(BF16).rearrange("p (a t) -> p a t", a=2)
            iters = [(hd, c, j, kb) for hd in range(8) for c in range(2) for j in range(tt + 1) for kb in range(4)]
            cur = dict(b=0)

            def emit_L(it):
                hd, c, j, kb = it
                m = 2 * hd + c
                if kb == 0:
                    b = cnt["b"] % 2
                    cnt["b"] += 1
                    cur["b"] = b
                    op("sp", lambda e: e.dma_start(out=KTb[b], in_=kv_d4[j, :, m, :]),
                       r=[("ktd", j)], w=[("KTb", b)], dma=f"ktb{b}")
                    op("sp", lambda e: e.dma_start(out=Vb[b].rearrange("p s e -> p (s e)"), in_=v_d4[j, :, hd, :]),
                       r=[("vd", j)], w=VbK[b], dma=f"vb{b}")
                b = cur["b"]
                diag = (j == tt)
                q0c = kb * 128 if diag else 0
                lb_ = cnt["l"] % 2
                pb = cnt["l"] % 2
                cnt["l"] += 1
                op("pe", lambda e: e.matmul(ps[lb_][:, q0c:TT], lhsT=KTb[b][:, kb * 128:(kb + 1) * 128], rhs=act[:, 16 + m, q0c:TT], start=True, stop=True),
                   r=[("KTb", b), AK(16 + m)], w=[PK(lb_)])
                op("act", lambda e: e.activation(out=probs[pb][:, q0c:TT], in_=ps[lb_][:, q0c:TT], func=AF.Exp, scale=QSCALE),
                   r=[PK(lb_)], w=[("probs", pb)])
                if diag:
                    op("pool", lambda e: e.tensor_tensor(out=probs[pb][:, kb * 128:(kb + 1) * 128], in0=probs[pb][:, kb * 128:(kb + 1) * 128],
                                                         in1=causal[:], op=ALU.mult), r=[("probs", pb), "causal"], w=[("probs", pb)])
                return dict(b=b, pb=pb, diag=diag)

            deferred = []

            def emit_O(it, info):
                hd, c, j, kb = it
                b, pb, diag = info["b"], info["pb"], info["diag"]
                for qb in range(kb if diag else 0, 4):
                    first = (j == 0 and kb == 0)
                    last = (diag and kb == qb)
                    op("pe", lambda e, qb=qb, first=first, last=last: e.matmul(
                        ps[obank[qb]][:, 0:257], lhsT=probs[pb][:, qb * 128:(qb + 1) * 128], rhs=Vb[b][:, kb, :], start=first, stop=last, skip_group_check=True),
                       r=[("probs", pb)] + VbK[b], w=[PK(obank[qb])])
                if not (j == tt and kb == 3):
                    return
                for qb in range(4):
                    ob = obank[qb]
                    op("dve", lambda e, qb=qb, ob=ob: e.reciprocal(out=rden[:, qb:qb + 1], in_=ps[ob][:, 256:257]), r=[PK(ob)], w=[("rden", qb)])
                    if c == 0:
                        op("dve", lambda e, qb=qb, ob=ob: e.tensor_scalar(out=o1[:, qb, :], in0=ps[ob][:, 0:256], scalar1=rden[:, qb:qb + 1], scalar2=None, op0=ALU.mult),
                           r=[PK(ob), ("rden", qb)], w=[("o1", qb)])
                    else:
                        op("dve", lambda e, qb=qb: e.tensor_scalar(out=rden[:, qb:qb + 1], in0=rden[:, qb:qb + 1], scalar1=NEGLAM, scalar2=None, op0=ALU.mult),
                           r=[("rden", qb), "small"], w=[("rden", qb)])
                        op("dve", lambda e, qb=qb, ob=ob: e.scalar_tensor_tensor(out=o1[:, qb, :], in0=ps[ob][:, 0:256], scalar=rden[:, qb:qb + 1], in1=o1[:, qb, :],
                                                                                 op0=ALU.mult, op1=ALU.add), r=[PK(ob), ("rden", qb), ("o1", qb)], w=[("o1", qb)])
                if c == 0:
                    return
                O1K = [("o1", qb) for qb in range(4)]
                op("act", lambda e: e.activation(out=o2, in_=o1, func=AF.Square), r=O1K, w=[("o2",)])
                op("dve", lambda e: e.reduce_sum(out=ssq, in_=o2, axis=AX.X), r=[("o2",)], w=[("ssq",)])
                op("act", lambda e: e.activation(out=ssq, in_=ssq, func=AF.Ln, scale=1.0 / 256, bias=epsc[:, 0:1]), r=[("ssq",), "epsc"], w=[("ssq",)])
                op("act", lambda e: e.activation(out=ssq, in_=ssq, func=AF.Exp, scale=-0.5), r=[("ssq",)], w=[("ssq",)])
                for qb in range(4):
                    op("dve", lambda e, qb=qb: e.scalar_tensor_tensor(out=yb[:, qb, :], in0=o1[:, qb, :], scalar=ssq[:, qb:qb + 1], in1=sublnb[:],
                                                                       op0=ALU.mult, op1=ALU.mult), r=[("o1", qb), ("ssq",), "sublnb"], w=[("yb", qb)])

                def tail(hd=hd):
                    for qb in range(4):
                        for e2 in range(2):
                            op("pe", lambda e, qb=qb, e2=e2: e.transpose(p7b[:, e2, qb * 128:(qb + 1) * 128], yb[:, qb, e2 * 128:(e2 + 1) * 128], ident[:]),
                               r=[("yb", qb), "ident"], w=[PK(7)])
                    op("act", lambda e: e.activation(out=act[:, 2 * hd:2 * hd + 2, :], in_=p7b, func=AF.Copy), r=[PK(7)], w=[AK(2 * hd), AK(2 * hd + 1)])
                deferred.append([3, tail])

            prev = None
            for it in iters:
                info = emit_L(it)
                if prev is not None:
                    emit_O(*prev)
                prev = (it, info)
                for dfr in list(deferred):
                    dfr[0] -= 1
                    if dfr[0] <= 0:
                        deferred.remove(dfr)
                        dfr[1]()
            emit_O(*prev)
            for dfr in deferred:
                dfr[1]()

        IDX = {nm: i for i, (nm, t) in reversed(list(enumerate(WLIST)))}
        XK = [("x", c) for c in range(NCH)]
        done = False
        for tt in range(NTT):
            t0 = tt * TT
            base = tt * NT
            op("sp", lambda e, t0=t0: e.dma_start(out=xT[:], in_=xT_v[:, :, t0:t0 + TT]), w=XK, dma="x")
            stages = []
            if stop_after != "load":
                norm_fm(G_MIX0)
            nh = {"load": 0, "norm": 0, "h1": 1}.get(stop_after, 16)
            hgrn_all(base + IDX["ainA"], nh)
            if nh == 16:
                linear_fm(base + IDX["aout"], 8, 2, 16, src_act, evac_resid)
            if stop_after not in ("mix0", "load", "norm", "h1"):
                mlp(base + IDX["up0"], G_MLP0)
            if stop_after not in ("mix0", "mlp0", "load", "norm", "h1"):
                ple(base + IDX["pp0"], G_PLE0, 0, t0)
            if stop_after is None or stop_after in ("mix1", "mlp1"):
                kv_stage(base + IDX["wk"], tt)
                attn_stage(base + IDX["bq"], tt)
                linear_fm(base + IDX["bout"], 8, 2, 16, src_act, evac_resid)
            if stop_after is None or stop_after == "mlp1":
                mlp(base + IDX["up1"], G_MLP1)
            if stop_after is None:
                ple(base + IDX["pp1"], G_PLE1, 1, t0)
            op("sp", lambda e, t0=t0: e.dma_start(out=out_v[:, :, t0:t0 + TT], in_=xT[:]), r=XK, dma="out")

        run = S_.emit(sem)
        with nc.Block() as block:
            @block.sync
            def _(eng):
                run("sp", eng)
                for s, v in S_.final_counts[1].items():
                    eng.wait_ge(sem(s), v)

            @block.scalar
            def _(eng):
                run("act", eng)

            @block.vector
            def _(eng):
                run("dve", eng)

            @block.gpsimd
            def _(eng):
                run("pool", eng)

            @block.tensor
            def _(eng):
                run("pe", eng)
    nc._n_ops = len(S_.ops)
    return nc


def host_prep(inputs):
    inp = {k: np.asarray(v) for k, v in inputs.items()}
    wf = host_weight_tiles(inp)
    vecs = host_vecs(inp)
    return inp, wf, vecs


def kernel(**inputs):
    inp, wf, vecs = host_prep(inputs)
    x = inp["x"]
    p = inp["p"]
    B, S, _ = x.shape
    nc = build_nc(S)
    in_maps = []
    for b in range(B):
        in_maps.append({
            "xT": np.ascontiguousarray(x[b].T),
            "pT": np.ascontiguousarray(p[:, b].transpose(0, 2, 1)),
            "wf": wf,
            "vecs": vecs,
        })
    res = run_bass_kernel_spmd(nc, in_maps, core_ids=list(range(B)))
    out = np.empty((B, S, D), np.float32)
    for b in range(B):
        out[b] = res.results[b]["outT"].T
    return out
```
